# Optimizing a Trainium2 kernel written in Bass

```python
import math
import jax, jax.numpy as jnp
from jax import lax
import numpy as np

D_MODEL = 1024
BATCH = 16
SEQ = 4096
DEPTH = 2
DEC_BATCH = 2
DEC_SEQ = 16384
PAST_LEN = 128

HG_HEADS = 4
HG_DK = 128
HG_DV = 128
HG_CHUNK = 64
DA_HEADS = 4
DA_DH = 64
DA_Q_BLOCK = 128
ROPE_THETA = 500000.0
ROPE_DIM = DA_DH // 4
NORM_EPS = 1e-6
SUBLN_EPS = 1e-5

HG_QK_WIDTH = HG_HEADS * HG_DK
HG_V_WIDTH = HG_HEADS * HG_DV
DA_QK_WIDTH = DA_HEADS * 2 * DA_DH
DA_V_WIDTH = DA_HEADS * 2 * DA_DH
IN_WIDTHS = (HG_QK_WIDTH, HG_QK_WIDTH, HG_QK_WIDTH, HG_V_WIDTH, HG_V_WIDTH,
             DA_QK_WIDTH, DA_QK_WIDTH, DA_V_WIDTH, DA_V_WIDTH, D_MODEL, D_MODEL)
D_IN = 6656

kernel_name = "hybrid_hgrn2_diffattn_gated_encoder"


def rms_norm(x, gain, eps=NORM_EPS):
    xf = x.astype(jnp.float32)
    y = xf * lax.rsqrt(jnp.mean(xf * xf, axis=-1, keepdims=True) + eps)
    return (y * gain.astype(jnp.float32)).astype(x.dtype)


def rotary_partial(t, positions):
    half = ROPE_DIM // 2
    inv_freq = jnp.power(ROPE_THETA, -jnp.arange(0, ROPE_DIM, 2, dtype=jnp.float32) / ROPE_DIM)
    ang = positions.astype(jnp.float32)[:, None] * inv_freq[None, :]
    cos = jnp.cos(ang).astype(t.dtype)
    sin = jnp.sin(ang).astype(t.dtype)
    t1 = t[..., :half]
    t2 = t[..., half:ROPE_DIM]
    return jnp.concatenate([t1 * cos - t2 * sin, t2 * cos + t1 * sin, t[..., ROPE_DIM:]], axis=-1)


def gla_chunk_scan(q, k, v, g):
    B, H, L, K = q.shape
    V = v.shape[-1]
    n_chunks = L // HG_CHUNK

    def to_chunks(t):
        return jnp.moveaxis(t.reshape(B, H, n_chunks, HG_CHUNK, t.shape[-1]), 2, 0)

    causal = jnp.tril(jnp.ones((HG_CHUNK, HG_CHUNK), dtype=bool))[:, :, None]

    def step(S, inp):
        qc, kc, vc, gc = inp
        b = jnp.cumsum(gc, axis=2)
        o_inter = jnp.einsum('bhck,bhkv->bhcv', qc * jnp.exp(b), S)
        diff = b[:, :, :, None, :] - b[:, :, None, :, :]
        decay = jnp.where(causal, jnp.exp(jnp.where(causal, diff, 0.0)), 0.0)
        scores = jnp.einsum('bhtk,bhsk,bhtsk->bhts', qc, kc, decay)
        o = o_inter + jnp.einsum('bhts,bhsv->bhtv', scores, vc)
        b_last = b[:, :, -1, :]
        S = jnp.exp(b_last)[..., None] * S + jnp.einsum(
            'bhck,bhcv->bhkv', kc * jnp.exp(b_last[:, :, None, :] - b), vc)
        return S, o

    S0 = jnp.zeros((B, H, K, V), dtype=jnp.float32)
    _, o = lax.scan(step, S0, (to_chunks(q), to_chunks(k), to_chunks(v), to_chunks(g)))
    return jnp.moveaxis(o, 0, 2).reshape(B, H, L, V)


def hgrn2_bidirectional(q_lin, f_fwd, f_bwd, i_lin, lb):
    B, L, _ = q_lin.shape

    def heads(t, d):
        return t.reshape(B, L, HG_HEADS, d).transpose(0, 2, 1, 3).astype(jnp.float32)

    q = heads(jax.nn.silu(q_lin.astype(jnp.float32)), HG_DK)
    v = heads(i_lin, HG_DV)

    def log_forget(z, lb_dir):
        f = lb_dir + (1.0 - lb_dir) * jax.nn.sigmoid(z.astype(jnp.float32))
        return heads(jnp.log(f), HG_DK)

    g_f = log_forget(f_fwd, lb[0])
    g_b = log_forget(f_bwd, lb[1])
    k_f = -jnp.expm1(g_f)
    k_b = -jnp.expm1(g_b)
    flip = lambda t: jnp.flip(t, axis=2)
    o_f = gla_chunk_scan(q, k_f, v, g_f)
    o_b = flip(gla_chunk_scan(flip(q), flip(k_b), flip(v), flip(g_b)))
    return o_f + o_b


def diff_attention(q_lin, k_lin, v_lin, lam, positions):
    B, L, _ = q_lin.shape
    q = q_lin.reshape(B, L, DA_HEADS, 2, DA_DH).transpose(0, 2, 3, 1, 4)
    k = k_lin.reshape(B, L, DA_HEADS, 2, DA_DH).transpose(0, 2, 3, 1, 4)
    v = v_lin.reshape(B, L, DA_HEADS, 2 * DA_DH).transpose(0, 2, 1, 3)
    q = rotary_partial(q, positions)
    k = rotary_partial(k, positions)
    n_blocks = L // DA_Q_BLOCK
    q_blocks = jnp.moveaxis(q.reshape(B, DA_HEADS, 2, n_blocks, DA_Q_BLOCK, DA_DH), 3, 0)
    scale = DA_DH ** -0.5

    def block(qb):
        s = jnp.einsum('bhmqd,bhmkd->bhmqk', qb, k, preferred_element_type=jnp.float32) * scale
        p = jax.nn.softmax(s, axis=-1)
        a = p[:, :, 0] - lam * p[:, :, 1]
        return jnp.einsum('bhqk,bhkv->bhqv', a.astype(v.dtype), v)

    o = lax.map(block, q_blocks)
    return jnp.moveaxis(o, 0, 2).reshape(B, DA_HEADS, L, 2 * DA_DH)


def hybrid_layer(x, c, layer_idx, lb, w_ada, b_ada, g_pre, g_post, w_in, hg_norm_g,
                 da_lambda, da_subln_g, w_proj_hg, w_proj_da, w_out):
    B, L, _ = x.shape
    positions = jnp.arange(L, dtype=jnp.int32)
    mod = jax.nn.silu(c) @ w_ada + b_ada
    shift, scale, gate = jnp.split(mod, 3, axis=-1)
    h = rms_norm(x, g_pre) * (1.0 + scale[:, None, :]) + shift[:, None, :]
    z = h @ w_in
    split_points = np.cumsum(IN_WIDTHS)[:-1].tolist()
    (hg_q, hg_ff, hg_fb, hg_i, hg_gate, da_q, da_k, da_v, da_gate, mg_hg, mg_da) = jnp.split(z, split_points, axis=-1)

    o_hg = hgrn2_bidirectional(hg_q, hg_ff, hg_fb, hg_i, lb)
    o_hg = rms_norm(o_hg, hg_norm_g).transpose(0, 2, 1, 3).reshape(B, L, HG_V_WIDTH)
    o_hg = o_hg.astype(x.dtype) * jax.nn.silu(hg_gate)

    lambda_init = 0.8 - 0.6 * math.exp(-0.3 * layer_idx)
    lf = da_lambda.astype(jnp.float32)
    lam = jnp.exp(jnp.sum(lf[0] * lf[1])) - jnp.exp(jnp.sum(lf[2] * lf[3])) + lambda_init
    o_da = diff_attention(da_q, da_k, da_v, lam, positions)
    o_da = rms_norm(o_da, da_subln_g, SUBLN_EPS) * (1.0 - lambda_init)
    o_da = o_da.transpose(0, 2, 1, 3).reshape(B, L, DA_V_WIDTH).astype(x.dtype) * jax.nn.silu(da_gate)

    merged = jax.nn.sigmoid(mg_hg) * (o_hg @ w_proj_hg) + jax.nn.sigmoid(mg_da) * (o_da @ w_proj_da)
    out = merged @ w_out
    return x + gate[:, None, :] * rms_norm(out, g_post)


def setup_inputs(seed: int = 0) -> dict:
    key = jax.random.key(seed)
    ks = jax.random.split(key, 16)
    f32 = jnp.float32
    s_d = D_MODEL ** -0.5
    return {
        "x_prompt": jax.random.normal(ks[0], (BATCH, SEQ, D_MODEL), f32),
        "x_sample": jax.random.normal(ks[1], (DEC_BATCH, DEC_SEQ, D_MODEL), f32),
        "c_prompt": jax.random.normal(ks[2], (BATCH, D_MODEL), f32),
        "c_sample": jax.random.normal(ks[3], (DEC_BATCH, D_MODEL), f32),
        "w_ada": jax.random.normal(ks[4], (DEPTH, D_MODEL, 3 * D_MODEL), f32) * s_d,
        "b_ada": jax.random.normal(ks[5], (DEPTH, 3 * D_MODEL), f32) * 0.01,
        "g_pre": 1.0 + 0.05 * jax.random.normal(ks[6], (DEPTH, D_MODEL), f32),
        "g_post": 1.0 + 0.05 * jax.random.normal(ks[7], (DEPTH, D_MODEL), f32),
        "w_in": jax.random.normal(ks[8], (DEPTH, D_MODEL, D_IN), f32) * s_d,
        "hg_lower_bounds": 0.1 * jax.random.normal(ks[9], (DEPTH, 2, HG_QK_WIDTH), f32),
        "hg_norm_g": 1.0 + 0.05 * jax.random.normal(ks[10], (DEPTH, HG_DV), f32),
        "da_lambda": 0.1 * jax.random.normal(ks[11], (DEPTH, 4, DA_DH), f32),
        "da_subln_g": 1.0 + 0.05 * jax.random.normal(ks[12], (DEPTH, 2 * DA_DH), f32),
        "w_proj_hg": jax.random.normal(ks[13], (DEPTH, HG_V_WIDTH, D_MODEL), f32) * HG_V_WIDTH ** -0.5,
        "w_proj_da": jax.random.normal(ks[14], (DEPTH, DA_V_WIDTH, D_MODEL), f32) * DA_V_WIDTH ** -0.5,
        "w_out": jax.random.normal(ks[15], (DEPTH, D_MODEL, D_MODEL), f32) * s_d,
    }


def reference(x_prompt, x_sample, c_prompt, c_sample, w_ada, b_ada, g_pre, g_post, w_in,
              hg_lower_bounds, hg_norm_g, da_lambda, da_subln_g, w_proj_hg, w_proj_da, w_out):
    lb_soft = jax.nn.softmax(hg_lower_bounds.astype(jnp.float32), axis=0)
    lb_all = jnp.cumsum(lb_soft, axis=0) - lb_soft[0]
    y_prompt = x_prompt
    y_sample = x_sample
    for l in range(DEPTH):
        y_prompt = hybrid_layer(y_prompt, c_prompt, l, lb_all[l], w_ada[l], b_ada[l], g_pre[l], g_post[l],
                                w_in[l], hg_norm_g[l], da_lambda[l], da_subln_g[l], w_proj_hg[l],
                                w_proj_da[l], w_out[l])
        y_sample = hybrid_layer(y_sample, c_sample, l, lb_all[l], w_ada[l], b_ada[l], g_pre[l], g_post[l],
                                w_in[l], hg_norm_g[l], da_lambda[l], da_subln_g[l], w_proj_hg[l],
                                w_proj_da[l], w_out[l])
    return (y_prompt, y_sample)
```

```python
import contextlib
import math
import numpy as np
import concourse.bass as bass
import concourse.mybir as mybir
from concourse.bass_utils import run_bass_kernel_spmd

F32 = mybir.dt.float32
BF16 = mybir.dt.bfloat16
AF = mybir.ActivationFunctionType
ALU = mybir.AluOpType
AX = mybir.AxisListType

D = 1024
DIN = 6656
DEPTH = 2
NORM_EPS = 1e-6
SUBLN_EPS = 1e-5
ROPE_THETA = 500000.0
N_CORES = 8


class FW:
    def __init__(self, nc, n_dma_sems=48):
        self.nc = nc
        self.engs = {"pe": nc.tensor, "act": nc.scalar, "dve": nc.vector, "pool": nc.gpsimd, "sp": nc.sync}
        self.sem = {k: nc.alloc_semaphore(name="s_" + k) for k in ("pe", "act", "dve", "pool")}
        self.cnt = {k: 0 for k in self.sem}
        self.dma_sems = [nc.alloc_semaphore(name=f"s_dma{i}") for i in range(n_dma_sems)]
        self.dma_cnt = [0] * n_dma_sems
        self.dma_rr = 0
        self.seen = {k: {} for k in self.engs}
        self.lastw = {}
        self.readers = {}

    def _wait(self, eng, tok):
        if tok is None:
            return
        key, sem, val = tok
        s = self.seen[eng]
        if s.get(key, 0) >= val:
            return
        self.engs[eng].wait_ge(sem, val)
        s[key] = val

    def _deps(self, eng, reads, writes):
        for r in reads:
            self._wait(eng, self.lastw.get(r))
        for w in writes:
            self._wait(eng, self.lastw.get(w))
            for k, tok in self.readers.get(w, {}).items():
                if k == eng:
                    continue
                self._wait(eng, tok)

    def _commit(self, tok, reads, writes):
        for w in writes:
            self.lastw[w] = tok
            self.readers[w] = {}
        for r in reads:
            if r in writes:
                continue
            self.readers.setdefault(r, {})[tok[0]] = tok

    def op(self, eng, fn, reads=(), writes=()):
        self._deps(eng, reads, writes)
        ins = fn()
        self.cnt[eng] += 1
        ins.then_inc(self.sem[eng], 1)
        tok = (eng, self.sem[eng], self.cnt[eng])
        self._commit(tok, reads, writes)
        return tok

    def dma(self, q, out, in_, reads=(), writes=(), **kw):
        i = self.dma_rr
        self.dma_rr = (self.dma_rr + 1) % len(self.dma_sems)
        sem = self.dma_sems[i]
        key = f"dma{i}"
        if self.dma_cnt[i] > 0:
            self._wait(q, (key, sem, self.dma_cnt[i]))
        self._deps(q, reads, writes)
        self.dma_cnt[i] += 16
        self.engs[q].dma_start(out=out, in_=in_, **kw).then_inc(sem, 16)
        tok = (key, sem, self.dma_cnt[i])
        self._commit(tok, reads, writes)
        return tok

    def finish(self, q):
        for i, sem in enumerate(self.dma_sems):
            if self.dma_cnt[i] > 0:
                self._wait(q, (f"dma{i}", sem, self.dma_cnt[i]))
        for k in self.sem:
            if self.cnt[k] > 0:
                self._wait(q, (k, self.sem[k], self.cnt[k]))

    def barrier(self):
        for q in self.engs:
            self.finish(q)


class Ring:
    def __init__(self, alloc, name, n, shape, dt):
        self.t = [alloc(f"{name}{i}", shape, dt) for i in range(n)]
        self.i = -1
        self.name = name

    def next(self):
        self.i = (self.i + 1) % len(self.t)
        return self.t[self.i], f"{self.name}{self.i}"


def lambda_init(l):
    return 0.8 - 0.6 * math.exp(-0.3 * l)


def host_consts(tmax):
    ident = np.eye(128, dtype=np.float32)
    idx = np.arange(128)
    ch = idx // 32
    loc = idx % 32
    same = (ch[:, None] == ch[None, :])
    t_, s_ = idx[:, None], idx[None, :]
    mats = np.zeros((6, 128, 128), np.float32)
    MQf = (same & (s_ <= t_)).astype(np.float32) - (same & (loc[None, :] <= 15)).astype(np.float32)
    MLf = (same & (s_ <= t_)).astype(np.float32)
    MKf = (same & (s_ > t_)).astype(np.float32)
    MQb = (same & (s_ >= t_)).astype(np.float32) - (same & (loc[None, :] >= 16)).astype(np.float32)
    MLb = (same & (s_ >= t_)).astype(np.float32)
    MKb = (same & (s_ < t_)).astype(np.float32)
    for i, m in enumerate((MQf, MLf, MKf, MQb, MLb, MKb)):
        mats[i] = m.T
    ind = np.zeros((128, 4), np.float32)
    ind[idx, ch] = 1.0
    masks = np.zeros((2, 128, 4, 32), np.float32)
    tl = np.arange(32)
    masks[0] = (tl[None, :] >= loc[:, None]).astype(np.float32)[:, None, :]
    masks[1] = (tl[None, :] <= loc[:, None]).astype(np.float32)[:, None, :]
    rope_dim = 16
    inv_freq = np.power(np.float32(ROPE_THETA), -np.arange(0, rope_dim, 2, dtype=np.float32) / np.float32(rope_dim)).astype(np.float32)
    pos = np.arange(tmax * 128, dtype=np.float32)
    ang = (pos[:, None] * inv_freq[None, :]).astype(np.float32)
    cos = np.cos(ang).astype(np.float32)
    sin = np.sin(ang).astype(np.float32)
    rope = np.zeros((tmax, 128, 128), np.float32)
    rope[:, :, 0:64] = np.tile(cos, (1, 8)).reshape(tmax, 128, 64)
    rope[:, :, 64:128] = np.tile(sin, (1, 8)).reshape(tmax, 128, 64)
    return dict(ident=ident, mats=mats, ind=ind, masks=masks.reshape(2, 128, 128), rope=rope)


def build(segs, nl=DEPTH, debug=False):
    nseg = len(segs)
    offs = [0]
    for L in segs:
        assert L % 512 == 0
        offs.append(offs[-1] + L)
    Ltot = offs[-1]
    tmax = max(segs) // 128
    nc = bass.Bass("TRN2", target_bir_lowering=False)

    def din(name, shape, dt=F32):
        return nc.dram_tensor(name, shape, dt, kind="ExternalInput").ap()

    def dscr(name, shape, dt):
        return nc.dram_tensor(name, shape, dt, kind="ExternalOutput" if debug else "Internal").ap()

    x_in = din("x", [Ltot, D])
    c_in = din("c", [nseg, 128, 8])
    w_ada = din("w_ada", [DEPTH, D, 3 * D])
    b_ada = din("b_ada", [DEPTH, 3 * D])
    g_pre = din("g_pre", [DEPTH, D])
    g_post = din("g_post", [DEPTH, D])
    w_in = din("w_in", [DEPTH, D, DIN])
    hg_lb = din("hg_lb", [1, DEPTH * 1024])
    hg_norm_g = din("hg_norm_g", [DEPTH, 128])
    da_lambda = din("da_lambda", [DEPTH, 256])
    da_subln_g = din("da_subln_g", [DEPTH, 128])
    w_proj_hg = din("w_proj_hg", [DEPTH, 512, D])
    w_proj_da = din("w_proj_da", [DEPTH, 512, D])
    w_out = din("w_out", [DEPTH, D, D])
    k_ident = din("k_ident", [128, 128])
    k_mats = din("k_mats", [6, 128, 128])
    k_ind = din("k_ind", [128, 4])
    k_masks = din("k_masks", [2, 128, 128])
    k_rope = din("k_rope", [tmax, 128, 128])
    y_out = nc.dram_tensor("y", [Ltot, D], F32, kind="ExternalOutput").ap()

    s_q = dscr("s_q", [Ltot, 512], BF16)
    s_f = [dscr("s_ff", [Ltot, 512], F32), dscr("s_fb", [Ltot, 512], F32)]
    s_v = dscr("s_v", [Ltot, 512], BF16)
    s_gh = dscr("s_gh", [Ltot, 512], BF16)
    s_QT = dscr("s_QT", [4, 128, Ltot], BF16)
    s_KT = dscr("s_KT", [4, 128, Ltot], BF16)
    s_GT = dscr("s_GT", [4, 128, Ltot], BF16)
    s_OT = dscr("s_OT", [4, 128, Ltot], BF16)
    s_V = dscr("s_V", [Ltot, 512], BF16)
    s_mgh = dscr("s_mgh", [Ltot, D], BF16)
    s_mgd = dscr("s_mgd", [Ltot, D], BF16)
    s_ob = dscr("s_ob", [Ltot, 512], F32)
    s_y0 = dscr("s_y0", [Ltot, D], F32)
    s_mod = dscr("s_mod", [DEPTH * nseg * 3, D], F32)

    fw = FW(nc)
    V, S_, P_, T = nc.vector, nc.scalar, nc.gpsimd, nc.tensor

    with contextlib.ExitStack() as top:
        def SB(name, shape, dt):
            return top.enter_context(nc.sbuf_tensor(name, shape, dt))

        identf = SB("identf", [128, 128], F32)
        identb = SB("identb", [128, 128], BF16)
        onesb = SB("onesb", [128, 128], BF16)
        mats = SB("mats", [128, 6, 128], F32)
        ind = SB("ind", [128, 4], F32)
        masks = SB("masks", [128, 2, 128], F32)
        LB = SB("LB", [128, DEPTH, 1024], F32)
        OMLB = SB("OMLB", [128, DEPTH, 1024], F32)
        neglam = SB("neglam", [128, DEPTH], F32)
        hgnb = SB("hgnb", [128, DEPTH, 512], F32)
        gsub = SB("gsub", [128, DEPTH], F32)

        with contextlib.ExitStack() as ph:
            def A(name, shape, dt):
                return ph.enter_context(nc.sbuf_tensor(name, shape, dt))

            fw.dma("sp", identf[:], k_ident[:, :], writes=["identf"])
            fw.dma("pool", identb[:], k_ident[:, :], writes=["identb"])
            fw.op("dve", lambda: V.memset(onesb[:], 1.0), writes=["onesb"])
            fw.dma("sp", mats[:], k_mats.rearrange("m p t -> p m t"), writes=["mats"])
            fw.dma("sp", ind[:], k_ind[:, :], writes=["ind"])
            fw.dma("sp", masks[:], k_masks.rearrange("d p c -> p d c"), writes=["masks"])
            lbraw = A("lbraw", [128, DEPTH * 1024], F32)
            lbe = A("lbe", [128, DEPTH * 1024], F32)
            lbt = A("lbt", [128, 4, 1024], F32)
            fw.dma("sp", lbraw[:], hg_lb[0:1, :].partition_broadcast(128), writes=["lbraw"])
            fw.op("act", lambda: S_.activation(out=lbe[:], in_=lbraw[:], func=AF.Exp), reads=["lbraw"], writes=["lbe"])
            fw.op("dve", lambda: V.tensor_tensor(lbt[:, 0, :], lbe[:, 0:1024], lbe[:, 1024:2048], ALU.add), reads=["lbe"], writes=["lbt0"])
            fw.op("dve", lambda: V.reciprocal(lbt[:, 1, :], lbt[:, 0, :]), reads=["lbt0"], writes=["lbt1"])
            fw.op("dve", lambda: V.tensor_tensor(lbt[:, 2, :], lbe[:, 0:1024], lbt[:, 1, :], ALU.mult), reads=["lbe", "lbt1"], writes=["lbt2"])
            fw.op("dve", lambda: V.tensor_tensor(lbt[:, 3, :], lbe[:, 1024:2048], lbt[:, 1, :], ALU.mult), reads=["lbe", "lbt1"], writes=["lbt3"])
            fw.op("dve", lambda: V.tensor_tensor(lbt[:, 0, :], lbt[:, 2, :], lbt[:, 3, :], ALU.add), reads=["lbt2", "lbt3", "lbt1"], writes=["lbt0"])
            fw.op("dve", lambda: V.tensor_tensor(LB[:, 0, :], lbt[:, 2, :], lbt[:, 2, :], ALU.subtract), reads=["lbt2"], writes=["LB0"])
            fw.op("dve", lambda: V.tensor_tensor(LB[:, 1, :], lbt[:, 0, :], lbt[:, 2, :], ALU.subtract), reads=["lbt0", "lbt2"], writes=["LB1"])
            fw.op("dve", lambda: V.tensor_scalar(OMLB[:], LB[:], -1.0, 1.0, ALU.mult, ALU.add), reads=["LB0", "LB1"], writes=["OMLB"])
            lamr = A("lamr", [128, DEPTH, 256], F32)
            lamp = A("lamp", [128, DEPTH, 2, 64], F32)
            lams = A("lams", [128, DEPTH, 4], F32)
            sgr = A("sgr", [128, DEPTH], F32)
            for l in range(DEPTH):
                fw.dma("sp", lamr[:, l, :], da_lambda[l:l + 1, :].partition_broadcast(128), writes=[f"lamr{l}"])
                fw.op("dve", lambda: V.tensor_tensor(lamp[:, l, 0, :], lamr[:, l, 0:64], lamr[:, l, 64:128], ALU.mult), reads=[f"lamr{l}"], writes=[f"lamp{l}a"])
                fw.op("dve", lambda: V.tensor_tensor(lamp[:, l, 1, :], lamr[:, l, 128:192], lamr[:, l, 192:256], ALU.mult), reads=[f"lamr{l}"], writes=[f"lamp{l}b"])
                fw.op("dve", lambda: V.reduce_sum(lams[:, l, 0:2], lamp[:, l, :, :], AX.X), reads=[f"lamp{l}a", f"lamp{l}b"], writes=[f"lams{l}"])
                fw.op("act", lambda: S_.activation(out=lams[:, l, 2:4], in_=lams[:, l, 0:2], func=AF.Exp), reads=[f"lams{l}"], writes=[f"lame{l}"])
                fw.op("dve", lambda: V.tensor_tensor(lams[:, l, 0:1], lams[:, l, 3:4], lams[:, l, 2:3], ALU.subtract), reads=[f"lame{l}"], writes=[f"lams{l}"])
                fw.op("dve", lambda: V.tensor_scalar(neglam[:, l:l + 1], lams[:, l, 0:1], -lambda_init(l), None, ALU.add), reads=[f"lams{l}"], writes=["neglam"])
                for h in range(4):
                    fw.dma("sp", hgnb[:, l, h * 128:(h + 1) * 128], hg_norm_g[l:l + 1, :].partition_broadcast(128), writes=[f"hgnb{l}{h}"])
                fw.dma("sp", sgr[:, l:l + 1], da_subln_g[l, :].rearrange("(p o) -> p o", o=1), writes=[f"sgr{l}"])
                fw.op("dve", lambda: V.tensor_scalar(gsub[:, l:l + 1], sgr[:, l:l + 1], 1.0 - lambda_init(l), None, ALU.mult), reads=[f"sgr{l}"], writes=["gsub"])

            wada = A("wada", [128, 8, 3 * D], BF16)
            badab = A("badab", [128, 3 * D], F32)
            gpreb = A("gpreb", [128, D], F32)
            gpostb = A("gpostb", [128, D], F32)
            cs = A("cs", [128, 8], F32)
            css = A("css", [128, 8], F32)
            cb = A("cb", [128, 8, 128], BF16)
            modt = A("modt", [128, 3 * D], F32)
            Gt = A("Gt", [128, D], F32)
            GPt = A("GPt", [128, D], F32)
            with nc.psum_tensor("pzm", [128, 512], F32) as pzm:
                for l in range(DEPTH):
                    for k in range(8):
                        fw.dma("pool", wada[:, k, :], w_ada[l, k * 128:(k + 1) * 128, :], writes=["wada"])
                    fw.dma("sp", badab[:], b_ada[l:l + 1, :].partition_broadcast(128), writes=["badab"])
                    fw.dma("sp", gpreb[:], g_pre[l:l + 1, :].partition_broadcast(128), writes=["gpreb"])
                    fw.dma("sp", gpostb[:], g_post[l:l + 1, :].partition_broadcast(128), writes=["gpostb"])
                    for sg in range(nseg):
                        fw.dma("sp", cs[:], c_in[sg, :, :], writes=["cs"])
                        fw.op("act", lambda: S_.activation(out=css[:], in_=cs[:], func=AF.Silu), reads=["cs"], writes=["css"])

                        def bc():
                            for k in range(8):
                                i = V.tensor_scalar(cb[:, k, :], onesb[:], css[:, k:k + 1], None, ALU.mult)
                            return i
                        fw.op("dve", bc, reads=["css", "onesb"], writes=["cb"])
                        for g in range(6):
                            def mm():
                                for k in range(8):
                                    i = T.matmul(pzm[:], lhsT=cb[:, k, :], rhs=wada[:, k, g * 512:(g + 1) * 512], start=(k == 0), stop=(k == 7))
                                return i
                            fw.op("pe", mm, reads=["cb", "wada"], writes=["pzm"])
                            fw.op("dve", lambda: V.tensor_tensor(modt[:, g * 512:(g + 1) * 512], pzm[:], badab[:, g * 512:(g + 1) * 512], ALU.add), reads=["pzm", "badab"], writes=["modt"])
                        fw.op("dve", lambda: V.scalar_tensor_tensor(Gt[:], modt[:, D:2 * D], 1.0, gpreb[:], ALU.add, ALU.mult), reads=["modt", "gpreb"], writes=["Gt"])
                        fw.op("dve", lambda: V.tensor_tensor(GPt[:], modt[:, 2 * D:3 * D], gpostb[:], ALU.mult), reads=["modt", "gpostb"], writes=["GPt"])
                        r0 = (l * nseg + sg) * 3
                        fw.dma("pool", s_mod[r0:r0 + 1, :], Gt[0:1, :], reads=["Gt"])
                        fw.dma("pool", s_mod[r0 + 1:r0 + 2, :], modt[0:1, 0:D], reads=["modt"])
                        fw.dma("pool", s_mod[r0 + 2:r0 + 3, :], GPt[0:1, :], reads=["GPt"])
            fw.barrier()

        for l in range(nl):
            xsrc = x_in if l == 0 else s_y0
            ydst = y_out if l == nl - 1 else s_y0

            with contextlib.ExitStack() as ph:
                def A(name, shape, dt):
                    return ph.enter_context(nc.sbuf_tensor(f"{name}_a{l}", shape, dt))

                def PS(name, shape, dt):
                    return ph.enter_context(nc.psum_tensor(f"{name}_a{l}", shape, dt))

                Win = A("Win", [128, 8, DIN], BF16)
                for k in range(8):
                    fw.dma("pool", Win[:, k, :], w_in[l, k * 128:(k + 1) * 128, :], writes=["Win"])
                Gb = A("Gb", [128, D], F32)
                SHb = A("SHb", [128, D], F32)
                xs = Ring(A, "xs", 2, [128, D], F32)
                junk = A("junk", [128, D], F32)
                ssq = A("ssq", [128, tmax], F32)
                rstd = A("rstd", [128, tmax], F32)
                hn = A("hn", [128, D], F32)
                hb = Ring(A, "hb", 2, [128, D], BF16)
                hT = Ring(A, "hT", 2, [128, 8, 128], BF16)
                rt = Ring(A, "rt", 2, [128, 128], F32)
                evf = Ring(A, "evf", 3, [128, 512], F32)
                evb = Ring(A, "evb", 6, [128, 512], BF16)
                rtmp = A("rtmp", [128, 4, 64], F32)
                tT = Ring(A, "tT", 2, [128, 4, 128], BF16)
                pT = PS("pT", [128, 8, 128], BF16)
                pz = Ring(PS, "pz", 4, [128, 512], F32)
                pT2 = Ring(PS, "pT2", 2, [128, 8, 128], BF16)

                for sg, L in enumerate(segs):
                    so = offs[sg]
                    nt = L // 128
                    r0 = (l * nseg + sg) * 3
                    fw.dma("sp", Gb[:], s_mod[r0:r0 + 1, :].partition_broadcast(128), writes=["Gb"])
                    fw.dma("sp", SHb[:], s_mod[r0 + 1:r0 + 2, :].partition_broadcast(128), writes=["SHb"])
                    fw.op("dve", lambda: V.memset(ssq[:], 0.0), writes=["ssq"])
                    for t in range(nt):
                        xt, xk = xs.next()
                        fw.dma("sp", xt[:], xsrc[so + t * 128: so + (t + 1) * 128, :], writes=[xk])
                        fw.op("act", lambda: S_.activation(out=junk[:], in_=xt[:], func=AF.Square, accum_out=ssq[:, t:t + 1]), reads=[xk, "ssq"], writes=["junk", "ssq"])
                    fw.op("dve", lambda: V.tensor_scalar(rstd[:, 0:nt], ssq[:, 0:nt], 1.0 / D, NORM_EPS, ALU.mult, ALU.add), reads=["ssq"], writes=["rstd"])
                    fw.op("act", lambda: S_.activation(out=rstd[:, 0:nt], in_=rstd[:, 0:nt], func=AF.Ln), reads=["rstd"], writes=["rstd"])
                    fw.op("act", lambda: S_.activation(out=rstd[:, 0:nt], in_=rstd[:, 0:nt], func=AF.Exp, scale=-0.5), reads=["rstd"], writes=["rstd"])

                    for t in range(nt):
                        tok = slice(so + t * 128, so + (t + 1) * 128)
                        xt, xk = xs.next()
                        fw.dma("sp", xt[:], xsrc[tok, :], writes=[xk])
                        rtt, rk = rt.next()
                        fw.dma("sp", rtt[:], k_rope[t, :, :], writes=[rk])
                        fw.op("dve", lambda: V.scalar_tensor_tensor(hn[:], xt[:], rstd[:, t:t + 1], Gb[:], ALU.mult, ALU.mult), reads=[xk, "rstd", "Gb"], writes=["hn"])
                        hbt, hbk = hb.next()
                        fw.op("dve", lambda: V.tensor_tensor(hbt[:], hn[:], SHb[:], ALU.add), reads=["hn", "SHb"], writes=[hbk])

                        def tr():
                            for k in range(8):
                                i = T.transpose(pT[:, k, :], hbt[:, k * 128:(k + 1) * 128], identb[:])
                            return i
                        fw.op("pe", tr, reads=[hbk, "identb"], writes=["pT"])
                        hTt, hTk = hT.next()
                        fw.op("act", lambda: S_.copy(hTt[:], pT[:]), reads=["pT"], writes=[hTk])

                        for g in range(13):
                            pzt, pzk = pz.next()

                            def mm():
                                for k in range(8):
                                    i = T.matmul(pzt[:], lhsT=hTt[:, k, :], rhs=Win[:, k, g * 512:(g + 1) * 512], start=(k == 0), stop=(k == 7))
                                return i
                            fw.op("pe", mm, reads=[hTk, "Win"], writes=[pzk])
                            if g in (0, 4):
                                eb, ek = evb.next()
                                fw.op("act", lambda: S_.activation(out=eb[:], in_=pzt[:], func=AF.Silu), reads=[pzk], writes=[ek])
                                fw.dma("pool", (s_q if g == 0 else s_gh)[tok, :], eb[:], reads=[ek])
                            elif g in (1, 2):
                                d_ = g - 1
                                ef, efk = evf.next()
                                fw.op("act", lambda: S_.activation(out=ef[:], in_=pzt[:], func=AF.Sigmoid), reads=[pzk], writes=[efk])
                                fw.op("dve", lambda: V.tensor_tensor(ef[:], ef[:], OMLB[:, l, d_ * 512:(d_ + 1) * 512], ALU.mult), reads=[efk, "OMLB"], writes=[efk])
                                fw.op("dve", lambda: V.tensor_tensor(ef[:], ef[:], LB[:, l, d_ * 512:(d_ + 1) * 512], ALU.add), reads=[efk, "LB0", "LB1"], writes=[efk])
                                fw.dma("pool", s_f[d_][tok, :], ef[:], reads=[efk])
                            elif g in (3, 7):
                                eb, ek = evb.next()
                                fw.op("act", lambda: S_.copy(eb[:], pzt[:]), reads=[pzk], writes=[ek])
                                fw.dma("pool", (s_v if g == 3 else s_V)[tok, :], eb[:], reads=[ek])
                            elif g in (5, 6):
                                ef, efk = evf.next()
                                fw.op("act", lambda: S_.copy(ef[:], pzt[:]), reads=[pzk], writes=[efk])
                                eb, ek = evb.next()
                                fw.op("dve", lambda: V.tensor_copy(eb[:], ef[:]), reads=[efk], writes=[ek])
                                v3 = ef[:].rearrange("p (m d) -> p m d", m=8)
                                b3 = eb[:].rearrange("p (m d) -> p m d", m=8)
                                cos3 = rtt[:, 0:64].rearrange("p (m j) -> p m j", m=8)
                                sin3 = rtt[:, 64:128].rearrange("p (m j) -> p m j", m=8)
                                tm = [rtmp[:, i, :].rearrange("p (m j) -> p m j", m=8) for i in range(4)]

                                def rope():
                                    V.tensor_tensor(tm[0], v3[:, :, 0:8], cos3, ALU.mult)
                                    V.tensor_tensor(tm[1], v3[:, :, 8:16], sin3, ALU.mult)
                                    V.tensor_tensor(tm[2], v3[:, :, 8:16], cos3, ALU.mult)
                                    return V.tensor_tensor(tm[3], v3[:, :, 0:8], sin3, ALU.mult)
                                fw.op("dve", rope, reads=[efk, rk], writes=["rtmp"])

                                def rope2():
                                    V.tensor_tensor(b3[:, :, 0:8], tm[0], tm[1], ALU.subtract)
                                    return V.tensor_tensor(b3[:, :, 8:16], tm[2], tm[3], ALU.add)
                                fw.op("dve", rope2, reads=["rtmp", ek], writes=[ek])
                                p2, p2k = pT2.next()

                                def tr2():
                                    for h in range(4):
                                        i = T.transpose(p2[:, h, :], eb[:, h * 128:(h + 1) * 128], identb[:])
                                    return i
                                fw.op("pe", tr2, reads=[ek, "identb"], writes=[p2k])
                                tt, tk = tT.next()
                                fw.op("act", lambda: S_.copy(tt[:], p2[:, 0:4, :]), reads=[p2k], writes=[tk])
                                dst = (s_QT if g == 5 else s_KT)
                                fw.dma("pool", dst[:, :, tok].rearrange("h p t -> p h t"), tt[:], reads=[tk])
                            elif g == 8:
                                eb, ek = evb.next()
                                fw.op("act", lambda: S_.activation(out=eb[:], in_=pzt[:], func=AF.Silu), reads=[pzk], writes=[ek])
                                p2, p2k = pT2.next()

                                def tr2():
                                    for h in range(4):
                                        i = T.transpose(p2[:, h, :], eb[:, h * 128:(h + 1) * 128], identb[:])
                                    return i
                                fw.op("pe", tr2, reads=[ek, "identb"], writes=[p2k])
                                tt, tk = tT.next()
                                fw.op("act", lambda: S_.copy(tt[:], p2[:, 0:4, :]), reads=[p2k], writes=[tk])
                                fw.dma("pool", s_GT[:, :, tok].rearrange("h p t -> p h t"), tt[:], reads=[tk])
                            else:
                                eb, ek = evb.next()
                                fw.op("act", lambda: S_.activation(out=eb[:], in_=pzt[:], func=AF.Sigmoid), reads=[pzk], writes=[ek])
                                dst = s_mgh if g < 11 else s_mgd
                                c0 = ((g - 9) % 2) * 512
                                fw.dma("pool", dst[tok, c0:c0 + 512], eb[:], reads=[ek])
                fw.barrier()

            with contextlib.ExitStack() as ph:
                def A(name, shape, dt):
                    return ph.enter_context(nc.sbuf_tensor(f"{name}_c{l}", shape, dt))

                def PS(name, shape, dt):
                    return ph.enter_context(nc.psum_tensor(f"{name}_c{l}", shape, dt))

                Lmax = max(segs)
                KTh = A("KTh", [128, Lmax], BF16)
                Vh = A("Vh", [128, Lmax // 128, 128], BF16)
                QTb = Ring(A, "QTb", 2, [128, 512], BF16)
                GTb = Ring(A, "GTb", 2, [128, 512], BF16)
                PT = Ring(A, "PT", 3, [128, 2, 512], BF16)
                rr = Ring(A, "rr", 2, [128, 512], F32)
                om = Ring(A, "om", 2, [128, 512], F32)
                oT = A("oT", [128, 512], F32)
                sq = A("sq", [128, 512], BF16)
                rn = A("rn", [128, 512], F32)
                of = A("of", [128, 512], F32)
                ofb = Ring(A, "ofb", 2, [128, 512], BF16)
                pS = Ring(PS, "pS", 2, [128, 2, 512], F32)
                pO = [PS("pO0", [128, 512], F32), PS("pO1", [128, 512], F32)]
                pSum = [PS("pSum0", [128, 512], F32), PS("pSum1", [128, 512], F32)]

                for sg, L in enumerate(segs):
                    so = offs[sg]
                    nt = L // 128
                    for h in range(4):
                        fw.dma("sp", KTh[:, 0:L], s_KT[h, :, so:so + L], writes=["KTh"])
                        for t0 in range(0, nt, 16):
                            t1 = min(nt, t0 + 16)
                            fw.dma("sp", Vh[:, t0:t1, :], s_V[so + t0 * 128:so + t1 * 128, h * 128:(h + 1) * 128].rearrange("(t p) c -> p t c", p=128), writes=["Vh"])
                        for qb in range(L // 512):
                            qs = slice(so + qb * 512, so + (qb + 1) * 512)
                            qt, qk = QTb.next()
                            fw.dma("sp", qt[:], s_QT[h, :, qs], writes=[qk])
                            gt, gk = GTb.next()
                            fw.dma("sp", gt[:], s_GT[h, :, qs], writes=[gk])
                            oms = []
                            for m in range(2):
                                pr = slice(64 * m, 64 * m + 64)
                                for j2 in range(nt // 2):
                                    pst, psk = pS.next()

                                    def qk_():
                                        for jj in range(2):
                                            j = j2 * 2 + jj
                                            i = T.matmul(pst[:, jj, :], lhsT=KTh[pr, j * 128:(j + 1) * 128], rhs=qt[pr, :], start=True, stop=True)
                                        return i
                                    fw.op("pe", qk_, reads=["KTh", qk], writes=[psk])
                                    ptt, ptk = PT.next()
                                    fw.op("act", lambda: S_.activation(out=ptt[:], in_=pst[:], func=AF.Exp, scale=0.125), reads=[psk], writes=[ptk])

                                    def pv_():
                                        for jj in range(2):
                                            j = j2 * 2 + jj
                                            T.matmul(pO[m][:], lhsT=Vh[:, j, :], rhs=ptt[:, jj, :], start=(j == 0), stop=(j == nt - 1))
                                            i = T.matmul(pSum[m][:], lhsT=onesb[:], rhs=ptt[:, jj, :], start=(j == 0), stop=(j == nt - 1))
                                        return i
                                    fw.op("pe", pv_, reads=["Vh", ptk, "onesb"], writes=[f"pO{m}", f"pSum{m}"])
                                rt_, rk_ = rr.next()
                                fw.op("dve", lambda: V.reciprocal(rt_[:], pSum[m][:]), reads=[f"pSum{m}"], writes=[rk_])
                                ot_, ok_ = om.next()
                                fw.op("dve", lambda: V.tensor_tensor(ot_[:], pO[m][:], rt_[:], ALU.mult), reads=[f"pO{m}", rk_], writes=[ok_])
                                oms.append((ot_, ok_))
                            (o0, o0k), (o1, o1k) = oms
                            fw.op("dve", lambda: V.scalar_tensor_tensor(oT[:], o1[:], neglam[:, l:l + 1], o0[:], ALU.mult, ALU.add), reads=[o0k, o1k, "neglam"], writes=["oT"])
                            fw.op("act", lambda: S_.activation(out=sq[:], in_=oT[:], func=AF.Square), reads=["oT"], writes=["sq"])
                            fw.op("pe", lambda: T.matmul(pSum[0][:], lhsT=onesb[:], rhs=sq[:], start=True, stop=True), reads=["sq", "onesb"], writes=["pSum0"])
                            fw.op("dve", lambda: V.tensor_scalar(rn[:], pSum[0][:], 1.0 / 128, SUBLN_EPS, ALU.mult, ALU.add), reads=["pSum0"], writes=["rn"])
                            fw.op("act", lambda: S_.activation(out=rn[:], in_=rn[:], func=AF.Ln), reads=["rn"], writes=["rn"])
                            fw.op("act", lambda: S_.activation(out=rn[:], in_=rn[:], func=AF.Exp, scale=-0.5), reads=["rn"], writes=["rn"])
                            fw.op("dve", lambda: V.tensor_tensor(of[:], oT[:], rn[:], ALU.mult), reads=["oT", "rn"], writes=["of"])
                            ob_, obk_ = ofb.next()
                            fw.op("dve", lambda: V.scalar_tensor_tensor(ob_[:], of[:], gsub[:, l:l + 1], gt[:], ALU.mult, ALU.mult), reads=["of", "gsub", gk], writes=[obk_])
                            fw.dma("pool", s_OT[h, :, qs], ob_[:], reads=[obk_])
                fw.barrier()

            with contextlib.ExitStack() as ph:
                def A(name, shape, dt):
                    return ph.enter_context(nc.sbuf_tensor(f"{name}_b{l}", shape, dt))

                def PS(name, shape, dt):
                    return ph.enter_context(nc.psum_tensor(f"{name}_b{l}", shape, dt))

                Wph = A("Wph", [128, 4, D], BF16)
                Wpd = A("Wpd", [128, 4, D], BF16)
                Wo = A("Wo", [128, 8, D], BF16)
                for k in range(4):
                    fw.dma("pool", Wph[:, k, :], w_proj_hg[l, k * 128:(k + 1) * 128, :], writes=["Wph"])
                    fw.dma("pool", Wpd[:, k, :], w_proj_da[l, k * 128:(k + 1) * 128, :], writes=["Wpd"])
                for k in range(8):
                    fw.dma("pool", Wo[:, k, :], w_out[l, k * 128:(k + 1) * 128, :], writes=["Wo"])
                GPb = A("GPb", [128, D], F32)
                Sst = [A("Sf", [128, 4, 128], F32), A("Sb", [128, 4, 128], F32)]
                Sbf = [A("Sbff", [128, 4, 128], BF16), A("Sbfb", [128, 4, 128], BF16)]
                qh = Ring(A, "qh", 2, [128, 512], BF16)
                fin = Ring(A, "fin", 2, [128, 512], F32)
                vh = Ring(A, "vh", 2, [128, 512], BF16)
                gl = Ring(A, "gl", 2, [128, 512], F32)
                kk = Ring(A, "kk", 2, [128, 512], F32)
                ex = Ring(A, "ex", 4, [128, 512], F32)
                dec = Ring(A, "dec", 2, [128, 16], F32)
                pbc = Ring(A, "pbc", 2, [128, 512], F32)
                qt_ = Ring(A, "qt_", 2, [128, 512], BF16)
                kt_ = Ring(A, "kt_", 2, [128, 512], BF16)
                qi_ = Ring(A, "qi_", 2, [128, 512], BF16)
                kh_ = Ring(A, "kh_", 2, [128, 512], BF16)
                trs = Ring(A, "trs", 6, [128, 4, 128], BF16)
                Asb = Ring(A, "Asb", 2, [128, 4, 32], BF16)
                obuf = Ring(A, "obuf", 2, [128, 512], F32)
                obt = Ring(A, "obt", 2, [128, 512], F32)
                osum = A("osum", [128, 512], F32)
                sq4 = A("sq4", [128, 128], F32)
                ss4 = A("ss4", [128, 8], F32)
                ghb = Ring(A, "ghb", 2, [128, 512], BF16)
                gg = A("gg", [128, 512], F32)
                og = A("og", [128, 512], BF16)
                ogT = A("ogT", [128, 4, 128], BF16)
                odT = Ring(A, "odT", 2, [128, 4, 128], BF16)
                mgh = Ring(A, "mgh", 2, [128, D], BF16)
                mgd = Ring(A, "mgd", 2, [128, D], BF16)
                m1 = A("m1", [128, D], F32)
                m2 = A("m2", [128, D], F32)
                mb = A("mb", [128, D], BF16)
                mT = A("mT", [128, 8, 128], BF16)
                junk2 = A("junk2", [128, D], F32)
                xr = Ring(A, "xr", 2, [128, D], F32)
                yt = Ring(A, "yt", 2, [128, D], F32)
                pb = [PS(f"pb{i}", [128, 512], F32) for i in range(3)]
                pA = PS("pA", [128, 4, 128], F32)
                po = PS("po", [128, 4, 128], F32)
                pSt = PS("pSt", [128, 4, 128], F32)
                pTr = PS("pTr", [128, 8, 128], BF16)
                pdec = PS("pdec", [128, 512], F32)

                def hg_tile(sg, so, t, d):
                    tok = slice(so + t * 128, so + (t + 1) * 128)
                    q_, qk_ = qh.next()
                    fw.dma("sp", q_[:], s_q[tok, :], writes=[qk_])
                    f_, fk_ = fin.next()
                    fw.dma("sp", f_[:], s_f[d][tok, :], writes=[fk_])
                    v_, vk_ = vh.next()
                    fw.dma("sp", v_[:], s_v[tok, :], writes=[vk_])
                    g_, gk_ = gl.next()
                    fw.op("act", lambda: S_.activation(out=g_[:], in_=f_[:], func=AF.Ln), reads=[fk_], writes=[gk_])
                    k_, kk_ = kk.next()
                    fw.op("dve", lambda: V.tensor_scalar(k_[:], f_[:], -1.0, 1.0, ALU.mult, ALU.add), reads=[fk_], writes=[kk_])
                    for i in range(3):
                        fw.op("pe", lambda: T.matmul(pb[i][:], lhsT=mats[:, 3 * d + i, :], rhs=g_[:], start=True, stop=True), reads=[gk_, "mats"], writes=[f"pb{i}"])

                    def dmm():
                        for h in range(4):
                            i_ = T.matmul(pdec[:, 4 * h:4 * h + 4], lhsT=g_[:, h * 128:(h + 1) * 128], rhs=ind[:], start=True, stop=True)
                        return i_
                    fw.op("pe", dmm, reads=[gk_, "ind"], writes=["pdec"])
                    es = [ex.next() for _ in range(4)]
                    pc_, pck_ = pbc.next()
                    fw.op("dve", lambda: V.tensor_scalar(pc_[:], pb[0][:], 43.0, -43.0, ALU.min, ALU.max), reads=["pb0"], writes=[pck_])
                    fw.op("act", lambda: S_.activation(out=es[0][0][:], in_=pc_[:], func=AF.Exp), reads=[pck_], writes=[es[0][1]])
                    fw.op("act", lambda: S_.activation(out=es[1][0][:], in_=pc_[:], func=AF.Exp, scale=-1.0), reads=[pck_], writes=[es[1][1]])
                    fw.op("act", lambda: S_.activation(out=es[2][0][:], in_=pb[1][:], func=AF.Exp), reads=["pb1"], writes=[es[2][1]])
                    fw.op("act", lambda: S_.activation(out=es[3][0][:], in_=pb[2][:], func=AF.Exp), reads=["pb2"], writes=[es[3][1]])
                    dc, dck = dec.next()
                    fw.op("act", lambda: S_.activation(out=dc[:], in_=pdec[:, 0:16], func=AF.Exp), reads=["pdec"], writes=[dck])
                    a_, ak_ = qt_.next()
                    fw.op("dve", lambda: V.tensor_tensor(a_[:], q_[:], es[0][0][:], ALU.mult), reads=[qk_, es[0][1]], writes=[ak_])
                    b_, bk_ = kt_.next()
                    fw.op("dve", lambda: V.tensor_tensor(b_[:], k_[:], es[1][0][:], ALU.mult), reads=[kk_, es[1][1]], writes=[bk_])
                    c_, ck_ = qi_.next()
                    fw.op("dve", lambda: V.tensor_tensor(c_[:], q_[:], es[2][0][:], ALU.mult), reads=[qk_, es[2][1]], writes=[ck_])
                    e_, ek_ = kh_.next()
                    fw.op("dve", lambda: V.tensor_tensor(e_[:], k_[:], es[3][0][:], ALU.mult), reads=[kk_, es[3][1]], writes=[ek_])
                    tts = []
                    for src, srck in ((a_, ak_), (b_, bk_), (c_, ck_)):
                        def tr():
                            for h in range(4):
                                i_ = T.transpose(pTr[:, h, :], src[:, h * 128:(h + 1) * 128], identb[:])
                            return i_
                        fw.op("pe", tr, reads=[srck, "identb"], writes=["pTr"])
                        tt, tk = trs.next()
                        fw.op("act", lambda: S_.copy(tt[:], pTr[:, 0:4, :]), reads=["pTr"], writes=[tk])
                        tts.append((tt, tk))
                    (qtT, qtTk), (ktT, ktTk), (qiT, qiTk) = tts
                    St, Sb_ = Sst[d], Sbf[d]
                    for c in ((0, 1, 2, 3) if d == 0 else (3, 2, 1, 0)):
                        rc = slice(32 * c, 32 * c + 32)

                        def sc():
                            for h in range(4):
                                i_ = T.matmul(pA[rc, h, 0:32], lhsT=ktT[:, h, rc], rhs=qtT[:, h, rc], start=True, stop=True, tile_position=(0, 32 * c))
                            return i_
                        fw.op("pe", sc, reads=[qtTk, ktTk], writes=["pA"])
                        at, atk = Asb.next()
                        fw.op("dve", lambda: V.tensor_tensor(at[rc, :, :], pA[rc, :, 0:32], masks[rc, d, :].rearrange("p (h t) -> p h t", h=4), ALU.mult), reads=["pA", "masks"], writes=[atk])

                        def om_():
                            for h in range(4):
                                T.matmul(po[rc, h, :], lhsT=at[rc, h, :], rhs=v_[rc, h * 128:(h + 1) * 128], start=True, stop=False, tile_position=(32 * c, 32 * c))
                                i_ = T.matmul(po[rc, h, :], lhsT=qiT[:, h, rc], rhs=Sb_[:, h, :], start=False, stop=True, tile_position=(0, 32 * c))
                            return i_
                        fw.op("pe", om_, reads=[atk, vk_, qiTk, f"Sbf{d}"], writes=["po"])

                        def sm_():
                            for h in range(4):
                                i_ = T.matmul(pSt[:, h, :], lhsT=e_[rc, h * 128:(h + 1) * 128], rhs=v_[rc, h * 128:(h + 1) * 128], start=True, stop=True, tile_position=(32 * c, 0))
                            return i_
                        fw.op("pe", sm_, reads=[ek_, vk_], writes=["pSt"])

                        def su_():
                            for h in range(4):
                                i_ = V.scalar_tensor_tensor(St[:, h, :], St[:, h, :], dc[:, 4 * h + c:4 * h + c + 1], pSt[:, h, :], ALU.mult, ALU.add)
                            return i_
                        fw.op("dve", su_, reads=["pSt", dck, f"S{d}"], writes=[f"S{d}"])
                        fw.op("act", lambda: S_.copy(Sb_[:], St[:]), reads=[f"S{d}"], writes=[f"Sbf{d}"])

                for sg, L in enumerate(segs):
                    so = offs[sg]
                    nt = L // 128
                    r0 = (l * nseg + sg) * 3
                    fw.dma("sp", GPb[:], s_mod[r0 + 2:r0 + 3, :].partition_broadcast(128), writes=["GPb"])
                    for d in range(2):
                        fw.op("dve", lambda: V.memset(Sst[d][:], 0.0), writes=[f"S{d}"])
                        fw.op("dve", lambda: V.memset(Sbf[d][:], 0.0), writes=[f"Sbf{d}"])
                    for t in reversed(range(nt)):
                        tok = slice(so + t * 128, so + (t + 1) * 128)
                        hg_tile(sg, so, t, 1)
                        ob_, obk_ = obuf.next()
                        fw.op("act", lambda: S_.copy(ob_[:], po[:].rearrange("p h v -> p (h v)")), reads=["po"], writes=[obk_])
                        fw.dma("pool", s_ob[tok, :], ob_[:], reads=[obk_], writes=[f"sob{so // 128 + t}"])
                    for t in range(nt):
                        tok = slice(so + t * 128, so + (t + 1) * 128)
                        hg_tile(sg, so, t, 0)
                        obl, oblk = obt.next()
                        fw.dma("sp", obl[:], s_ob[tok, :], reads=[f"sob{so // 128 + t}"], writes=[oblk])
                        fw.op("dve", lambda: V.tensor_tensor(osum[:], po[:].rearrange("p h v -> p (h v)"), obl[:], ALU.add), reads=["po", oblk], writes=["osum"])
                        fw.op("dve", lambda: V.memset(ss4[:, 0:4], 0.0), writes=["ss4"])

                        def sqs():
                            for h in range(4):
                                i_ = S_.activation(out=sq4[:], in_=osum[:, h * 128:(h + 1) * 128], func=AF.Square, accum_out=ss4[:, h:h + 1])
                            return i_
                        fw.op("act", sqs, reads=["osum", "ss4"], writes=["sq4", "ss4"])
                        fw.op("dve", lambda: V.tensor_scalar(ss4[:, 4:8], ss4[:, 0:4], 1.0 / 128, NORM_EPS, ALU.mult, ALU.add), reads=["ss4"], writes=["ss4b"])
                        fw.op("act", lambda: S_.activation(out=ss4[:, 4:8], in_=ss4[:, 4:8], func=AF.Ln), reads=["ss4b"], writes=["ss4b"])
                        fw.op("act", lambda: S_.activation(out=ss4[:, 4:8], in_=ss4[:, 4:8], func=AF.Exp, scale=-0.5), reads=["ss4b"], writes=["ss4b"])
                        gh_, ghk_ = ghb.next()
                        fw.dma("sp", gh_[:], s_gh[tok, :], writes=[ghk_])
                        fw.op("dve", lambda: V.tensor_tensor(gg[:], gh_[:], hgnb[:, l, :], ALU.mult), reads=[ghk_] + [f"hgnb{l}{h}" for h in range(4)], writes=["gg"])

                        def ogf():
                            for h in range(4):
                                i_ = V.scalar_tensor_tensor(og[:, h * 128:(h + 1) * 128], osum[:, h * 128:(h + 1) * 128], ss4[:, 4 + h:5 + h], gg[:, h * 128:(h + 1) * 128], ALU.mult, ALU.mult)
                            return i_
                        fw.op("dve", ogf, reads=["osum", "ss4b", "gg"], writes=["og"])

                        def tr():
                            for h in range(4):
                                i_ = T.transpose(pTr[:, h, :], og[:, h * 128:(h + 1) * 128], identb[:])
                            return i_
                        fw.op("pe", tr, reads=["og", "identb"], writes=["pTr"])
                        fw.op("act", lambda: S_.copy(ogT[:], pTr[:, 0:4, :]), reads=["pTr"], writes=["ogT"])
                        od_, odk_ = odT.next()
                        fw.dma("sp", od_[:], s_OT[:, :, tok].rearrange("h p t -> p h t"), writes=[odk_])
                        mh_, mhk_ = mgh.next()
                        fw.dma("sp", mh_[:], s_mgh[tok, :], writes=[mhk_])
                        md_, mdk_ = mgd.next()
                        fw.dma("sp", md_[:], s_mgd[tok, :], writes=[mdk_])
                        for (lt, ltk, W_, Wk_, mg_, mgk_, mo, mok) in ((ogT, "ogT", Wph, "Wph", mh_, mhk_, m1, "m1"), (od_, odk_, Wpd, "Wpd", md_, mdk_, m2, "m2")):
                            def pm():
                                for n in range(2):
                                    for h in range(4):
                                        i_ = T.matmul(pb[n][:], lhsT=lt[:, h, :], rhs=W_[:, h, n * 512:(n + 1) * 512], start=(h == 0), stop=(h == 3))
                                return i_
                            fw.op("pe", pm, reads=[ltk, Wk_], writes=["pb0", "pb1"])

                            def gm():
                                for n in range(2):
                                    i_ = V.tensor_tensor(mo[:, n * 512:(n + 1) * 512], pb[n][:], mg_[:, n * 512:(n + 1) * 512], ALU.mult)
                                return i_
                            fw.op("dve", gm, reads=["pb0", "pb1", mgk_], writes=[mok])
                        fw.op("dve", lambda: V.tensor_tensor(mb[:], m1[:], m2[:], ALU.add), reads=["m1", "m2"], writes=["mb"])

                        def tr8():
                            for k in range(8):
                                i_ = T.transpose(pTr[:, k, :], mb[:, k * 128:(k + 1) * 128], identb[:])
                            return i_
                        fw.op("pe", tr8, reads=["mb", "identb"], writes=["pTr"])
                        fw.op("act", lambda: S_.copy(mT[:], pTr[:]), reads=["pTr"], writes=["mT"])

                        def om2():
                            for n in range(2):
                                for k in range(8):
                                    i_ = T.matmul(pb[n][:], lhsT=mT[:, k, :], rhs=Wo[:, k, n * 512:(n + 1) * 512], start=(k == 0), stop=(k == 7))
                            return i_
                        fw.op("pe", om2, reads=["mT", "Wo"], writes=["pb0", "pb1"])
                        fw.op("dve", lambda: V.memset(ss4[:, 0:2], 0.0), writes=["ss4"])

                        def sq2():
                            for n in range(2):
                                i_ = S_.activation(out=junk2[:, n * 512:(n + 1) * 512], in_=pb[n][:], func=AF.Square, accum_out=ss4[:, n:n + 1])
                            return i_
                        fw.op("act", sq2, reads=["pb0", "pb1", "ss4"], writes=["junk2", "ss4"])
                        fw.op("dve", lambda: V.tensor_tensor(ss4[:, 2:3], ss4[:, 0:1], ss4[:, 1:2], ALU.add), reads=["ss4"], writes=["ss4c"])
                        fw.op("dve", lambda: V.tensor_scalar(ss4[:, 2:3], ss4[:, 2:3], 1.0 / D, NORM_EPS, ALU.mult, ALU.add), reads=["ss4c"], writes=["ss4c"])
                        fw.op("act", lambda: S_.activation(out=ss4[:, 2:3], in_=ss4[:, 2:3], func=AF.Ln), reads=["ss4c"], writes=["ss4c"])
                        fw.op("act", lambda: S_.activation(out=ss4[:, 3:4], in_=ss4[:, 2:3], func=AF.Exp, scale=-0.5), reads=["ss4c"], writes=["ss4d"])
                        x_, xk_ = xr.next()
                        fw.dma("sp", x_[:], xsrc[tok, :], writes=[xk_])
                        y_, yk_ = yt.next()

                        def yo():
                            for n in range(2):
                                i_ = V.scalar_tensor_tensor(y_[:, n * 512:(n + 1) * 512], pb[n][:], ss4[:, 3:4], GPb[:, n * 512:(n + 1) * 512], ALU.mult, ALU.mult)
                            return i_
                        fw.op("dve", yo, reads=["pb0", "pb1", "ss4d", "GPb"], writes=[yk_])
                        fw.op("dve", lambda: V.tensor_tensor(y_[:], y_[:], x_[:], ALU.add), reads=[yk_, xk_], writes=[yk_])
                        fw.dma("pool", ydst[tok, :], y_[:], reads=[yk_])
                fw.barrier()
    return nc


_CACHE = {}


def run(segs, x_cores, c_cores, weights, n_cores=N_CORES, nl=DEPTH, debug=False):
    key = (tuple(segs), nl, debug)
    if key not in _CACHE:
        _CACHE[key] = (build(list(segs), nl, debug), host_consts(max(segs) // 128))
    nc, kc = _CACHE[key]
    f = lambda a: np.ascontiguousarray(np.asarray(a, dtype=np.float32))
    common = {
        "w_ada": f(weights["w_ada"]), "b_ada": f(weights["b_ada"]), "g_pre": f(weights["g_pre"]), "g_post": f(weights["g_post"]),
        "w_in": f(weights["w_in"]), "hg_lb": f(weights["hg_lower_bounds"]).reshape(1, -1), "hg_norm_g": f(weights["hg_norm_g"]),
        "da_lambda": f(weights["da_lambda"]).reshape(DEPTH, 256), "da_subln_g": f(weights["da_subln_g"]),
        "w_proj_hg": f(weights["w_proj_hg"]), "w_proj_da": f(weights["w_proj_da"]), "w_out": f(weights["w_out"]),
        "k_ident": kc["ident"], "k_mats": kc["mats"], "k_ind": kc["ind"], "k_masks": kc["masks"], "k_rope": kc["rope"],
    }
    in_maps = []
    for i in range(n_cores):
        c = f(c_cores[i])
        c_lay = np.ascontiguousarray(c.reshape(len(segs), 8, 128).transpose(0, 2, 1))
        in_maps.append({"x": f(x_cores[i]), "c": c_lay, **common})
    res = run_bass_kernel_spmd(nc, in_maps, core_ids=list(range(n_cores)))
    if debug:
        return res.results
    return [r["y"] for r in res.results]


def kernel(x_prompt, x_sample, c_prompt, c_sample, w_ada, b_ada, g_pre, g_post, w_in, hg_lower_bounds, hg_norm_g,
           da_lambda, da_subln_g, w_proj_hg, w_proj_da, w_out):
    x_prompt = np.asarray(x_prompt, np.float32)
    x_sample = np.asarray(x_sample, np.float32)
    c_prompt = np.asarray(c_prompt, np.float32)
    c_sample = np.asarray(c_sample, np.float32)
    B, L, _ = x_prompt.shape
    Bs, Ls, _ = x_sample.shape
    per = B // N_CORES
    segs = [L] * per + [Ls]
    weights = dict(w_ada=w_ada, b_ada=b_ada, g_pre=g_pre, g_post=g_post, w_in=w_in, hg_lower_bounds=hg_lower_bounds,
                   hg_norm_g=hg_norm_g, da_lambda=da_lambda, da_subln_g=da_subln_g, w_proj_hg=w_proj_hg,
                   w_proj_da=w_proj_da, w_out=w_out)
    xc, cc = [], []
    for i in range(N_CORES):
        xs = [x_prompt[i * per + j] for j in range(per)] + [x_sample[i % Bs]]
        cs = [c_prompt[i * per + j] for j in range(per)] + [c_sample[i % Bs]]
        xc.append(np.concatenate(xs, axis=0))
        cc.append(np.stack(cs, axis=0))
    ys = run(segs, xc, cc, weights)
    y_prompt = np.empty_like(x_prompt)
    y_sample = np.empty_like(x_sample)
    for i in range(N_CORES):
        for j in range(per):
            y_prompt[i * per + j] = ys[i][j * L:(j + 1) * L]
    for s in range(Bs):
        y_sample[s] = ys[s][per * L: per * L + Ls]
    return (y_prompt, y_sample)
```

```python
import contextlib
import math
import numpy as np
import concourse.bass as bass
import concourse.mybir as mybir
from concourse.bass_utils import run_bass_kernel_spmd

F32 = mybir.dt.float32
BF16 = mybir.dt.bfloat16
AF = mybir.ActivationFunctionType
ALU = mybir.AluOpType
AX = mybir.AxisListType

D = 1024
DIN = 6656
DEPTH = 2
NORM_EPS = 1e-6
SUBLN_EPS = 1e-5
ROPE_THETA = 500000.0
N_CORES = 8


class FW:
    def __init__(self, nc, n_dma_sems=48):
        self.nc = nc
        self.engs = {"pe": nc.tensor, "act": nc.scalar, "dve": nc.vector, "pool": nc.gpsimd, "sp": nc.sync}
        self.sem = {k: nc.alloc_semaphore(name="s_" + k) for k in ("pe", "act", "dve", "pool")}
        self.cnt = {k: 0 for k in self.sem}
        self.dma_sems = [nc.alloc_semaphore(name=f"s_dma{i}") for i in range(n_dma_sems)]
        self.dma_cnt = [0] * n_dma_sems
        self.dma_rr = 0
        self.seen = {k: {} for k in self.engs}
        self.lastw = {}
        self.readers = {}

    def _wait(self, eng, tok):
        if tok is None:
            return
        key, sem, val = tok
        s = self.seen[eng]
        if s.get(key, 0) >= val:
            return
        self.engs[eng].wait_ge(sem, val)
        s[key] = val

    def _deps(self, eng, reads, writes):
        for r in reads:
            self._wait(eng, self.lastw.get(r))
        for w in writes:
            self._wait(eng, self.lastw.get(w))
            for k, tok in self.readers.get(w, {}).items():
                if k == eng:
                    continue
                self._wait(eng, tok)

    def _commit(self, tok, reads, writes):
        for w in writes:
            self.lastw[w] = tok
            self.readers[w] = {}
        for r in reads:
            if r in writes:
                continue
            self.readers.setdefault(r, {})[tok[0]] = tok

    def op(self, eng, fn, reads=(), writes=()):
        self._deps(eng, reads, writes)
        ins = fn()
        self.cnt[eng] += 1
        ins.then_inc(self.sem[eng], 1)
        tok = (eng, self.sem[eng], self.cnt[eng])
        self._commit(tok, reads, writes)
        return tok

    def dma(self, q, out, in_, reads=(), writes=(), **kw):
        i = self.dma_rr
        self.dma_rr = (self.dma_rr + 1) % len(self.dma_sems)
        sem = self.dma_sems[i]
        key = f"dma{i}"
        if self.dma_cnt[i] > 0:
            self._wait(q, (key, sem, self.dma_cnt[i]))
        self._deps(q, reads, writes)
        self.dma_cnt[i] += 16
        self.engs[q].dma_start(out=out, in_=in_, **kw).then_inc(sem, 16)
        tok = (key, sem, self.dma_cnt[i])
        self._commit(tok, reads, writes)
        return tok

    def finish(self, q):
        for i, sem in enumerate(self.dma_sems):
            if self.dma_cnt[i] > 0:
                self._wait(q, (f"dma{i}", sem, self.dma_cnt[i]))
        for k in self.sem:
            if self.cnt[k] > 0:
                self._wait(q, (k, self.sem[k], self.cnt[k]))

    def barrier(self):
        for q in self.engs:
            self.finish(q)


class Ring:
    def __init__(self, alloc, name, n, shape, dt):
        self.t = [alloc(f"{name}{i}", shape, dt) for i in range(n)]
        self.i = -1
        self.name = name

    def next(self):
        self.i = (self.i + 1) % len(self.t)
        return self.t[self.i], f"{self.name}{self.i}"


def lambda_init(l):
    return 0.8 - 0.6 * math.exp(-0.3 * l)


def host_consts(tmax):
    ident = np.eye(128, dtype=np.float32)
    idx = np.arange(128)
    ch = idx // 32
    loc = idx % 32
    same = (ch[:, None] == ch[None, :])
    t_, s_ = idx[:, None], idx[None, :]
    mats = np.zeros((6, 128, 128), np.float32)
    MQf = (same & (s_ <= t_)).astype(np.float32) - (same & (loc[None, :] <= 15)).astype(np.float32)
    MLf = (same & (s_ <= t_)).astype(np.float32)
    MKf = (same & (s_ > t_)).astype(np.float32)
    MQb = (same & (s_ >= t_)).astype(np.float32) - (same & (loc[None, :] >= 16)).astype(np.float32)
    MLb = (same & (s_ >= t_)).astype(np.float32)
    MKb = (same & (s_ < t_)).astype(np.float32)
    for i, m in enumerate((MQf, MLf, MKf, MQb, MLb, MKb)):
        mats[i] = m.T
    ind = np.zeros((128, 4), np.float32)
    ind[idx, ch] = 1.0
    masks = np.zeros((2, 128, 4, 32), np.float32)
    tl = np.arange(32)
    masks[0] = (tl[None, :] >= loc[:, None]).astype(np.float32)[:, None, :]
    masks[1] = (tl[None, :] <= loc[:, None]).astype(np.float32)[:, None, :]
    rope_dim = 16
    inv_freq = np.power(np.float32(ROPE_THETA), -np.arange(0, rope_dim, 2, dtype=np.float32) / np.float32(rope_dim)).astype(np.float32)
    pos = np.arange(tmax * 128, dtype=np.float32)
    ang = (pos[:, None] * inv_freq[None, :]).astype(np.float32)
    cos = np.cos(ang).astype(np.float32)
    sin = np.sin(ang).astype(np.float32)
    rope = np.zeros((tmax, 128, 128), np.float32)
    rope[:, :, 0:64] = np.tile(cos, (1, 8)).reshape(tmax, 128, 64)
    rope[:, :, 64:128] = np.tile(sin, (1, 8)).reshape(tmax, 128, 64)
    return dict(ident=ident, mats=mats, ind=ind, masks=masks.reshape(2, 128, 128), rope=rope)


def build(segs, nl=DEPTH, debug=False):
    nseg = len(segs)
    offs = [0]
    for L in segs:
        assert L % 512 == 0
        offs.append(offs[-1] + L)
    Ltot = offs[-1]
    tmax = max(segs) // 128
    nc = bass.Bass("TRN2", target_bir_lowering=False)

    def din(name, shape, dt=F32):
        return nc.dram_tensor(name, shape, dt, kind="ExternalInput").ap()

    def dscr(name, shape, dt):
        return nc.dram_tensor(name, shape, dt, kind="ExternalOutput" if debug else "Internal").ap()

    x_in = din("x", [Ltot, D])
    c_in = din("c", [nseg, 128, 8])
    w_ada = din("w_ada", [DEPTH, D, 3 * D])
    b_ada = din("b_ada", [DEPTH, 3 * D])
    g_pre = din("g_pre", [DEPTH, D])
    g_post = din("g_post", [DEPTH, D])
    w_in = din("w_in", [DEPTH, D, DIN])
    hg_lb = din("hg_lb", [1, DEPTH * 1024])
    hg_norm_g = din("hg_norm_g", [DEPTH, 128])
    da_lambda = din("da_lambda", [DEPTH, 256])
    da_subln_g = din("da_subln_g", [DEPTH, 128])
    w_proj_hg = din("w_proj_hg", [DEPTH, 512, D])
    w_proj_da = din("w_proj_da", [DEPTH, 512, D])
    w_out = din("w_out", [DEPTH, D, D])
    k_ident = din("k_ident", [128, 128])
    k_mats = din("k_mats", [6, 128, 128])
    k_ind = din("k_ind", [128, 4])
    k_masks = din("k_masks", [2, 128, 128])
    k_rope = din("k_rope", [tmax, 128, 128])
    y_out = nc.dram_tensor("y", [Ltot, D], F32, kind="ExternalOutput").ap()

    s_q = dscr("s_q", [Ltot, 512], BF16)
    s_f = [dscr("s_ff", [Ltot, 512], F32), dscr("s_fb", [Ltot, 512], F32)]
    s_v = dscr("s_v", [Ltot, 512], BF16)
    s_gh = dscr("s_gh", [Ltot, 512], BF16)
    s_QT = dscr("s_QT", [4, 128, Ltot], BF16)
    s_KT = dscr("s_KT", [4, 128, Ltot], BF16)
    s_GT = dscr("s_GT", [4, 128, Ltot], BF16)
    s_OT = dscr("s_OT", [4, 128, Ltot], BF16)
    s_V = dscr("s_V", [Ltot, 512], BF16)
    s_mgh = dscr("s_mgh", [Ltot, D], BF16)
    s_mgd = dscr("s_mgd", [Ltot, D], BF16)
    s_ob = dscr("s_ob", [Ltot, 512], F32)
    s_y0 = dscr("s_y0", [Ltot, D], F32)
    s_mod = dscr("s_mod", [DEPTH * nseg * 3, D], F32)

    fw = FW(nc)
    V, S_, P_, T = nc.vector, nc.scalar, nc.gpsimd, nc.tensor

    with contextlib.ExitStack() as top:
        def SB(name, shape, dt):
            return top.enter_context(nc.sbuf_tensor(name, shape, dt))

        identf = SB("identf", [128, 128], F32)
        identb = SB("identb", [128, 128], BF16)
        onesb = SB("onesb", [128, 128], BF16)
        mats = SB("mats", [128, 6, 128], F32)
        ind = SB("ind", [128, 4], F32)
        masks = SB("masks", [128, 2, 128], F32)
        LB = SB("LB", [128, DEPTH, 1024], F32)
        OMLB = SB("OMLB", [128, DEPTH, 1024], F32)
        neglam = SB("neglam", [128, DEPTH], F32)
        hgnb = SB("hgnb", [128, DEPTH, 512], F32)
        gsub = SB("gsub", [128, DEPTH], F32)

        with contextlib.ExitStack() as ph:
            def A(name, shape, dt):
                return ph.enter_context(nc.sbuf_tensor(name, shape, dt))

            fw.dma("sp", identf[:], k_ident[:, :], writes=["identf"])
            fw.dma("pool", identb[:], k_ident[:, :], writes=["identb"])
            fw.op("dve", lambda: V.memset(onesb[:], 1.0), writes=["onesb"])
            fw.dma("sp", mats[:], k_mats.rearrange("m p t -> p m t"), writes=["mats"])
            fw.dma("sp", ind[:], k_ind[:, :], writes=["ind"])
            fw.dma("sp", masks[:], k_masks.rearrange("d p c -> p d c"), writes=["masks"])
            lbraw = A("lbraw", [128, DEPTH * 1024], F32)
            lbe = A("lbe", [128, DEPTH * 1024], F32)
            lbt = A("lbt", [128, 4, 1024], F32)
            fw.dma("sp", lbraw[:], hg_lb[0:1, :].partition_broadcast(128), writes=["lbraw"])
            fw.op("act", lambda: S_.activation(out=lbe[:], in_=lbraw[:], func=AF.Exp), reads=["lbraw"], writes=["lbe"])
            fw.op("dve", lambda: V.tensor_tensor(lbt[:, 0, :], lbe[:, 0:1024], lbe[:, 1024:2048], ALU.add), reads=["lbe"], writes=["lbt0"])
            fw.op("dve", lambda: V.reciprocal(lbt[:, 1, :], lbt[:, 0, :]), reads=["lbt0"], writes=["lbt1"])
            fw.op("dve", lambda: V.tensor_tensor(lbt[:, 2, :], lbe[:, 0:1024], lbt[:, 1, :], ALU.mult), reads=["lbe", "lbt1"], writes=["lbt2"])
            fw.op("dve", lambda: V.tensor_tensor(lbt[:, 3, :], lbe[:, 1024:2048], lbt[:, 1, :], ALU.mult), reads=["lbe", "lbt1"], writes=["lbt3"])
            fw.op("dve", lambda: V.tensor_tensor(lbt[:, 0, :], lbt[:, 2, :], lbt[:, 3, :], ALU.add), reads=["lbt2", "lbt3", "lbt1"], writes=["lbt0"])
            fw.op("dve", lambda: V.tensor_tensor(LB[:, 0, :], lbt[:, 2, :], lbt[:, 2, :], ALU.subtract), reads=["lbt2"], writes=["LB0"])
            fw.op("dve", lambda: V.tensor_tensor(LB[:, 1, :], lbt[:, 0, :], lbt[:, 2, :], ALU.subtract), reads=["lbt0", "lbt2"], writes=["LB1"])
            fw.op("dve", lambda: V.tensor_scalar(OMLB[:], LB[:], -1.0, 1.0, ALU.mult, ALU.add), reads=["LB0", "LB1"], writes=["OMLB"])
            lamr = A("lamr", [128, DEPTH, 256], F32)
            lamp = A("lamp", [128, DEPTH, 2, 64], F32)
            lams = A("lams", [128, DEPTH, 4], F32)
            sgr = A("sgr", [128, DEPTH], F32)
            for l in range(DEPTH):
                fw.dma("sp", lamr[:, l, :], da_lambda[l:l + 1, :].partition_broadcast(128), writes=[f"lamr{l}"])
                fw.op("dve", lambda: V.tensor_tensor(lamp[:, l, 0, :], lamr[:, l, 0:64], lamr[:, l, 64:128], ALU.mult), reads=[f"lamr{l}"], writes=[f"lamp{l}a"])
                fw.op("dve", lambda: V.tensor_tensor(lamp[:, l, 1, :], lamr[:, l, 128:192], lamr[:, l, 192:256], ALU.mult), reads=[f"lamr{l}"], writes=[f"lamp{l}b"])
                fw.op("dve", lambda: V.reduce_sum(lams[:, l, 0:2], lamp[:, l, :, :], AX.X), reads=[f"lamp{l}a", f"lamp{l}b"], writes=[f"lams{l}"])
                fw.op("act", lambda: S_.activation(out=lams[:, l, 2:4], in_=lams[:, l, 0:2], func=AF.Exp), reads=[f"lams{l}"], writes=[f"lame{l}"])
                fw.op("dve", lambda: V.tensor_tensor(lams[:, l, 0:1], lams[:, l, 3:4], lams[:, l, 2:3], ALU.subtract), reads=[f"lame{l}"], writes=[f"lams{l}"])
                fw.op("dve", lambda: V.tensor_scalar(neglam[:, l:l + 1], lams[:, l, 0:1], -lambda_init(l), None, ALU.add), reads=[f"lams{l}"], writes=["neglam"])
                for h in range(4):
                    fw.dma("sp", hgnb[:, l, h * 128:(h + 1) * 128], hg_norm_g[l:l + 1, :].partition_broadcast(128), writes=[f"hgnb{l}{h}"])
                fw.dma("sp", sgr[:, l:l + 1], da_subln_g[l, :].rearrange("(p o) -> p o", o=1), writes=[f"sgr{l}"])
                fw.op("dve", lambda: V.tensor_scalar(gsub[:, l:l + 1], sgr[:, l:l + 1], 1.0 - lambda_init(l), None, ALU.mult), reads=[f"sgr{l}"], writes=["gsub"])

            wada = A("wada", [128, 8, 3 * D], BF16)
            badab = A("badab", [128, 3 * D], F32)
            gpreb = A("gpreb", [128, D], F32)
            gpostb = A("gpostb", [128, D], F32)
            cs = A("cs", [128, 8], F32)
            css = A("css", [128, 8], F32)
            cb = A("cb", [128, 8, 128], BF16)
            modt = A("modt", [128, 3 * D], F32)
            Gt = A("Gt", [128, D], F32)
            GPt = A("GPt", [128, D], F32)
            with nc.psum_tensor("pzm", [128, 512], F32) as pzm:
                for l in range(DEPTH):
                    for k in range(8):
                        fw.dma("pool", wada[:, k, :], w_ada[l, k * 128:(k + 1) * 128, :], writes=["wada"])
                    fw.dma("sp", badab[:], b_ada[l:l + 1, :].partition_broadcast(128), writes=["badab"])
                    fw.dma("sp", gpreb[:], g_pre[l:l + 1, :].partition_broadcast(128), writes=["gpreb"])
                    fw.dma("sp", gpostb[:], g_post[l:l + 1, :].partition_broadcast(128), writes=["gpostb"])
                    for sg in range(nseg):
                        fw.dma("sp", cs[:], c_in[sg, :, :], writes=["cs"])
                        fw.op("act", lambda: S_.activation(out=css[:], in_=cs[:], func=AF.Silu), reads=["cs"], writes=["css"])

                        def bc():
                            for k in range(8):
                                i = V.tensor_scalar(cb[:, k, :], onesb[:], css[:, k:k + 1], None, ALU.mult)
                            return i
                        fw.op("dve", bc, reads=["css", "onesb"], writes=["cb"])
                        for g in range(6):
                            def mm():
                                for k in range(8):
                                    i = T.matmul(pzm[:], lhsT=cb[:, k, :], rhs=wada[:, k, g * 512:(g + 1) * 512], start=(k == 0), stop=(k == 7))
                                return i
                            fw.op("pe", mm, reads=["cb", "wada"], writes=["pzm"])
                            fw.op("dve", lambda: V.tensor_tensor(modt[:, g * 512:(g + 1) * 512], pzm[:], badab[:, g * 512:(g + 1) * 512], ALU.add), reads=["pzm", "badab"], writes=["modt"])
                        fw.op("dve", lambda: V.scalar_tensor_tensor(Gt[:], modt[:, D:2 * D], 1.0, gpreb[:], ALU.add, ALU.mult), reads=["modt", "gpreb"], writes=["Gt"])
                        fw.op("dve", lambda: V.tensor_tensor(GPt[:], modt[:, 2 * D:3 * D], gpostb[:], ALU.mult), reads=["modt", "gpostb"], writes=["GPt"])
                        r0 = (l * nseg + sg) * 3
                        fw.dma("pool", s_mod[r0:r0 + 1, :], Gt[0:1, :], reads=["Gt"])
                        fw.dma("pool", s_mod[r0 + 1:r0 + 2, :], modt[0:1, 0:D], reads=["modt"])
                        fw.dma("pool", s_mod[r0 + 2:r0 + 3, :], GPt[0:1, :], reads=["GPt"])
            fw.barrier()

        for l in range(nl):
            xsrc = x_in if l == 0 else s_y0
            ydst = y_out if l == nl - 1 else s_y0

            with contextlib.ExitStack() as ph:
                def A(name, shape, dt):
                    return ph.enter_context(nc.sbuf_tensor(f"{name}_a{l}", shape, dt))

                def PS(name, shape, dt):
                    return ph.enter_context(nc.psum_tensor(f"{name}_a{l}", shape, dt))

                Win = A("Win", [128, 8, DIN], BF16)
                for k in range(8):
                    fw.dma("pool", Win[:, k, :], w_in[l, k * 128:(k + 1) * 128, :], writes=["Win"])
                Gb = A("Gb", [128, D], F32)
                SHb = A("SHb", [128, D], F32)
                xs = Ring(A, "xs", 2, [128, D], F32)
                junk = A("junk", [128, D], F32)
                ssq = A("ssq", [128, tmax], F32)
                rstd = A("rstd", [128, tmax], F32)
                hn = A("hn", [128, D], F32)
                hb = Ring(A, "hb", 2, [128, D], BF16)
                hT = Ring(A, "hT", 2, [128, 8, 128], BF16)
                rt = Ring(A, "rt", 2, [128, 128], F32)
                evf = Ring(A, "evf", 3, [128, 512], F32)
                evb = Ring(A, "evb", 6, [128, 512], BF16)
                rtmp = A("rtmp", [128, 4, 64], F32)
                tT = Ring(A, "tT", 2, [128, 4, 128], BF16)
                pT = PS("pT", [128, 8, 128], BF16)
                pz = Ring(PS, "pz", 4, [128, 512], F32)
                pT2 = Ring(PS, "pT2", 2, [128, 8, 128], BF16)

                for sg, L in enumerate(segs):
                    so = offs[sg]
                    nt = L // 128
                    r0 = (l * nseg + sg) * 3
                    fw.dma("sp", Gb[:], s_mod[r0:r0 + 1, :].partition_broadcast(128), writes=["Gb"])
                    fw.dma("sp", SHb[:], s_mod[r0 + 1:r0 + 2, :].partition_broadcast(128), writes=["SHb"])
                    fw.op("dve", lambda: V.memset(ssq[:], 0.0), writes=["ssq"])
                    for t in range(nt):
                        xt, xk = xs.next()
                        fw.dma("sp", xt[:], xsrc[so + t * 128: so + (t + 1) * 128, :], writes=[xk])
                        fw.op("act", lambda: S_.activation(out=junk[:], in_=xt[:], func=AF.Square, accum_out=ssq[:, t:t + 1]), reads=[xk, "ssq"], writes=["junk", "ssq"])
                    fw.op("dve", lambda: V.tensor_scalar(rstd[:, 0:nt], ssq[:, 0:nt], 1.0 / D, NORM_EPS, ALU.mult, ALU.add), reads=["ssq"], writes=["rstd"])
                    fw.op("act", lambda: S_.activation(out=rstd[:, 0:nt], in_=rstd[:, 0:nt], func=AF.Ln), reads=["rstd"], writes=["rstd"])
                    fw.op("act", lambda: S_.activation(out=rstd[:, 0:nt], in_=rstd[:, 0:nt], func=AF.Exp, scale=-0.5), reads=["rstd"], writes=["rstd"])

                    for t in range(nt):
                        tok = slice(so + t * 128, so + (t + 1) * 128)
                        xt, xk = xs.next()
                        fw.dma("sp", xt[:], xsrc[tok, :], writes=[xk])
                        rtt, rk = rt.next()
                        fw.dma("sp", rtt[:], k_rope[t, :, :], writes=[rk])
                        fw.op("dve", lambda: V.scalar_tensor_tensor(hn[:], xt[:], rstd[:, t:t + 1], Gb[:], ALU.mult, ALU.mult), reads=[xk, "rstd", "Gb"], writes=["hn"])
                        hbt, hbk = hb.next()
                        fw.op("dve", lambda: V.tensor_tensor(hbt[:], hn[:], SHb[:], ALU.add), reads=["hn", "SHb"], writes=[hbk])

                        def tr():
                            for k in range(8):
                                i = T.transpose(pT[:, k, :], hbt[:, k * 128:(k + 1) * 128], identb[:])
                            return i
                        fw.op("pe", tr, reads=[hbk, "identb"], writes=["pT"])
                        hTt, hTk = hT.next()
                        fw.op("act", lambda: S_.copy(hTt[:], pT[:]), reads=["pT"], writes=[hTk])

                        for g in range(13):
                            pzt, pzk = pz.next()

                            def mm():
                                for k in range(8):
                                    i = T.matmul(pzt[:], lhsT=hTt[:, k, :], rhs=Win[:, k, g * 512:(g + 1) * 512], start=(k == 0), stop=(k == 7))
                                return i
                            fw.op("pe", mm, reads=[hTk, "Win"], writes=[pzk])
                            if g in (0, 4):
                                eb, ek = evb.next()
                                fw.op("act", lambda: S_.activation(out=eb[:], in_=pzt[:], func=AF.Silu), reads=[pzk], writes=[ek])
                                fw.dma("pool", (s_q if g == 0 else s_gh)[tok, :], eb[:], reads=[ek])
                            elif g in (1, 2):
                                d_ = g - 1
                                ef, efk = evf.next()
                                fw.op("act", lambda: S_.activation(out=ef[:], in_=pzt[:], func=AF.Sigmoid), reads=[pzk], writes=[efk])
                                fw.op("dve", lambda: V.tensor_tensor(ef[:], ef[:], OMLB[:, l, d_ * 512:(d_ + 1) * 512], ALU.mult), reads=[efk, "OMLB"], writes=[efk])
                                fw.op("dve", lambda: V.tensor_tensor(ef[:], ef[:], LB[:, l, d_ * 512:(d_ + 1) * 512], ALU.add), reads=[efk, "LB0", "LB1"], writes=[efk])
                                fw.dma("pool", s_f[d_][tok, :], ef[:], reads=[efk])
                            elif g in (3, 7):
                                eb, ek = evb.next()
                                fw.op("act", lambda: S_.copy(eb[:], pzt[:]), reads=[pzk], writes=[ek])
                                fw.dma("pool", (s_v if g == 3 else s_V)[tok, :], eb[:], reads=[ek])
                            elif g in (5, 6):
                                ef, efk = evf.next()
                                fw.op("act", lambda: S_.copy(ef[:], pzt[:]), reads=[pzk], writes=[efk])
                                eb, ek = evb.next()
                                fw.op("dve", lambda: V.tensor_copy(eb[:], ef[:]), reads=[efk], writes=[ek])
                                v3 = ef[:].rearrange("p (m d) -> p m d", m=8)
                                b3 = eb[:].rearrange("p (m d) -> p m d", m=8)
                                cos3 = rtt[:, 0:64].rearrange("p (m j) -> p m j", m=8)
                                sin3 = rtt[:, 64:128].rearrange("p (m j) -> p m j", m=8)
                                tm = [rtmp[:, i, :].rearrange("p (m j) -> p m j", m=8) for i in range(4)]

                                def rope():
                                    V.tensor_tensor(tm[0], v3[:, :, 0:8], cos3, ALU.mult)
                                    V.tensor_tensor(tm[1], v3[:, :, 8:16], sin3, ALU.mult)
                                    V.tensor_tensor(tm[2], v3[:, :, 8:16], cos3, ALU.mult)
                                    return V.tensor_tensor(tm[3], v3[:, :, 0:8], sin3, ALU.mult)
                                fw.op("dve", rope, reads=[efk, rk], writes=["rtmp"])

                                def rope2():
                                    V.tensor_tensor(b3[:, :, 0:8], tm[0], tm[1], ALU.subtract)
                                    return V.tensor_tensor(b3[:, :, 8:16], tm[2], tm[3], ALU.add)
                                fw.op("dve", rope2, reads=["rtmp", ek], writes=[ek])
                                p2, p2k = pT2.next()

                                def tr2():
                                    for h in range(4):
                                        i = T.transpose(p2[:, h, :], eb[:, h * 128:(h + 1) * 128], identb[:])
                                    return i
                                fw.op("pe", tr2, reads=[ek, "identb"], writes=[p2k])
                                tt, tk = tT.next()
                                fw.op("act", lambda: S_.copy(tt[:], p2[:, 0:4, :]), reads=[p2k], writes=[tk])
                                dst = (s_QT if g == 5 else s_KT)
                                fw.dma("pool", dst[:, :, tok].rearrange("h p t -> p h t"), tt[:], reads=[tk])
                            elif g == 8:
                                eb, ek = evb.next()
                                fw.op("act", lambda: S_.activation(out=eb[:], in_=pzt[:], func=AF.Silu), reads=[pzk], writes=[ek])
                                p2, p2k = pT2.next()

                                def tr2():
                                    for h in range(4):
                                        i = T.transpose(p2[:, h, :], eb[:, h * 128:(h + 1) * 128], identb[:])
                                    return i
                                fw.op("pe", tr2, reads=[ek, "identb"], writes=[p2k])
                                tt, tk = tT.next()
                                fw.op("act", lambda: S_.copy(tt[:], p2[:, 0:4, :]), reads=[p2k], writes=[tk])
                                fw.dma("pool", s_GT[:, :, tok].rearrange("h p t -> p h t"), tt[:], reads=[tk])
                            else:
                                eb, ek = evb.next()
                                fw.op("act", lambda: S_.activation(out=eb[:], in_=pzt[:], func=AF.Sigmoid), reads=[pzk], writes=[ek])
                                dst = s_mgh if g < 11 else s_mgd
                                c0 = ((g - 9) % 2) * 512
                                fw.dma("pool", dst[tok, c0:c0 + 512], eb[:], reads=[ek])
                fw.barrier()

            with contextlib.ExitStack() as ph:
                def A(name, shape, dt):
                    return ph.enter_context(nc.sbuf_tensor(f"{name}_c{l}", shape, dt))

                def PS(name, shape, dt):
                    return ph.enter_context(nc.psum_tensor(f"{name}_c{l}", shape, dt))

                Lmax = max(segs)
                KTh = A("KTh", [128, Lmax], BF16)
                Vh = A("Vh", [128, Lmax // 128, 128], BF16)
                QTb = Ring(A, "QTb", 2, [128, 512], BF16)
                GTb = Ring(A, "GTb", 2, [128, 512], BF16)
                PT = Ring(A, "PT", 3, [128, 2, 512], BF16)
                rr = Ring(A, "rr", 2, [128, 512], F32)
                om = Ring(A, "om", 2, [128, 512], F32)
                oT = A("oT", [128, 512], F32)
                sq = A("sq", [128, 512], BF16)
                rn = A("rn", [128, 512], F32)
                of = A("of", [128, 512], F32)
                ofb = Ring(A, "ofb", 2, [128, 512], BF16)
                pS = Ring(PS, "pS", 3, [128, 2, 512], F32)
                pO = [PS("pO0", [128, 512], F32), PS("pO1", [128, 512], F32)]
                accT = A("accT", [128, 2, 512], F32)
                onesf = A("onesf", [128, 128], F32)
                fw.op("dve", lambda: V.memset(onesf[:], 1.0), writes=["onesf"])

                for sg, L in enumerate(segs):
                    so = offs[sg]
                    nt = L // 128
                    for h in range(4):
                        fw.dma("sp", KTh[:, 0:L], s_KT[h, :, so:so + L], writes=["KTh"])
                        for t0 in range(0, nt, 16):
                            t1 = min(nt, t0 + 16)
                            fw.dma("sp", Vh[:, t0:t1, :], s_V[so + t0 * 128:so + t1 * 128, h * 128:(h + 1) * 128].rearrange("(t p) c -> p t c", p=128), writes=["Vh"])
                        for qb in range(L // 512):
                            qs = slice(so + qb * 512, so + (qb + 1) * 512)
                            qt, qk = QTb.next()
                            fw.dma("sp", qt[:], s_QT[h, :, qs], writes=[qk])
                            gt, gk = GTb.next()
                            fw.dma("sp", gt[:], s_GT[h, :, qs], writes=[gk])

                            def emit_qk(j):
                                pst, psk = pS.next()

                                def qk_():
                                    T.matmul(pst[:, 0, :], lhsT=KTh[0:64, j * 128:(j + 1) * 128], rhs=qt[0:64, :], start=True, stop=True, tile_position=(0, 0))
                                    return T.matmul(pst[:, 1, :], lhsT=KTh[64:128, j * 128:(j + 1) * 128], rhs=qt[64:128, :], start=True, stop=True, tile_position=(64, 0))
                                fw.op("pe", qk_, reads=["KTh", qk], writes=[psk])
                                return pst, psk

                            pend = emit_qk(0)
                            for j in range(nt):
                                pst, psk = pend
                                if j + 1 < nt:
                                    pend = emit_qk(j + 1)
                                ptt, ptk = PT.next()
                                fw.op("act", lambda: S_.activation(out=ptt[:], in_=pst[:], func=AF.Exp, scale=0.125), reads=[psk], writes=[ptk])

                                def pv_():
                                    T.matmul(pO[0][:], lhsT=Vh[:, j, :], rhs=ptt[:, 0, :], start=(j == 0), stop=(j == nt - 1))
                                    return T.matmul(pO[1][:], lhsT=Vh[:, j, :], rhs=ptt[:, 1, :], start=(j == 0), stop=(j == nt - 1))
                                fw.op("pe", pv_, reads=["Vh", ptk], writes=["pO0", "pO1"])
                                if j == 0:
                                    fw.op("dve", lambda: V.tensor_copy(accT[:], ptt[:]), reads=[ptk], writes=["acc"])
                                else:
                                    fw.op("dve", lambda: V.tensor_tensor(accT[:], accT[:], ptt[:], ALU.add), reads=[ptk, "acc"], writes=["acc"])
                            oms = []
                            for m in range(2):
                                pst, psk = pS.next()
                                fw.op("pe", lambda: T.matmul(pst[:, 0, :], lhsT=onesf[:], rhs=accT[:, m, :], start=True, stop=True), reads=["onesf", "acc"], writes=[psk])
                                rt_, rk_ = rr.next()
                                fw.op("dve", lambda: V.reciprocal(rt_[:], pst[:, 0, :]), reads=[psk], writes=[rk_])
                                ot_, ok_ = om.next()
                                fw.op("dve", lambda: V.tensor_tensor(ot_[:], pO[m][:], rt_[:], ALU.mult), reads=[f"pO{m}", rk_], writes=[ok_])
                                oms.append((ot_, ok_))
                            (o0, o0k), (o1, o1k) = oms
                            fw.op("dve", lambda: V.scalar_tensor_tensor(oT[:], o1[:], neglam[:, l:l + 1], o0[:], ALU.mult, ALU.add), reads=[o0k, o1k, "neglam"], writes=["oT"])
                            fw.op("act", lambda: S_.activation(out=sq[:], in_=oT[:], func=AF.Square), reads=["oT"], writes=["sq"])
                            pst, psk = pS.next()
                            fw.op("pe", lambda: T.matmul(pst[:, 0, :], lhsT=onesb[:], rhs=sq[:], start=True, stop=True), reads=["sq", "onesb"], writes=[psk])
                            fw.op("dve", lambda: V.tensor_scalar(rn[:], pst[:, 0, :], 1.0 / 128, SUBLN_EPS, ALU.mult, ALU.add), reads=[psk], writes=["rn"])
                            fw.op("act", lambda: S_.activation(out=rn[:], in_=rn[:], func=AF.Ln), reads=["rn"], writes=["rn"])
                            fw.op("act", lambda: S_.activation(out=rn[:], in_=rn[:], func=AF.Exp, scale=-0.5), reads=["rn"], writes=["rn"])
                            fw.op("dve", lambda: V.tensor_tensor(of[:], oT[:], rn[:], ALU.mult), reads=["oT", "rn"], writes=["of"])
                            ob_, obk_ = ofb.next()
                            fw.op("dve", lambda: V.scalar_tensor_tensor(ob_[:], of[:], gsub[:, l:l + 1], gt[:], ALU.mult, ALU.mult), reads=["of", "gsub", gk], writes=[obk_])
                            fw.dma("pool", s_OT[h, :, qs], ob_[:], reads=[obk_])
                fw.barrier()

            with contextlib.ExitStack() as ph:
                def A(name, shape, dt):
                    return ph.enter_context(nc.sbuf_tensor(f"{name}_b{l}", shape, dt))

                def PS(name, shape, dt):
                    return ph.enter_context(nc.psum_tensor(f"{name}_b{l}", shape, dt))

                Wph = A("Wph", [128, 4, D], BF16)
                Wpd = A("Wpd", [128, 4, D], BF16)
                Wo = A("Wo", [128, 8, D], BF16)
                for k in range(4):
                    fw.dma("pool", Wph[:, k, :], w_proj_hg[l, k * 128:(k + 1) * 128, :], writes=["Wph"])
                    fw.dma("pool", Wpd[:, k, :], w_proj_da[l, k * 128:(k + 1) * 128, :], writes=["Wpd"])
                for k in range(8):
                    fw.dma("pool", Wo[:, k, :], w_out[l, k * 128:(k + 1) * 128, :], writes=["Wo"])
                GPb = A("GPb", [128, D], F32)
                Sst = [A("Sf", [128, 4, 128], F32), A("Sb", [128, 4, 128], F32)]
                Sbf = [A("Sbff", [128, 4, 128], BF16), A("Sbfb", [128, 4, 128], BF16)]
                qh = Ring(A, "qh", 2, [128, 512], BF16)
                fin = Ring(A, "fin", 2, [128, 512], F32)
                vh = Ring(A, "vh", 2, [128, 512], BF16)
                gl = Ring(A, "gl", 2, [128, 512], F32)
                kk = Ring(A, "kk", 2, [128, 512], F32)
                ex = Ring(A, "ex", 4, [128, 512], F32)
                dec = Ring(A, "dec", 2, [128, 16], F32)
                pbc = Ring(A, "pbc", 2, [128, 512], F32)
                qt_ = Ring(A, "qt_", 2, [128, 512], BF16)
                kt_ = Ring(A, "kt_", 2, [128, 512], BF16)
                qi_ = Ring(A, "qi_", 2, [128, 512], BF16)
                kh_ = Ring(A, "kh_", 2, [128, 512], BF16)
                trs = Ring(A, "trs", 6, [128, 4, 128], BF16)
                Asb = Ring(A, "Asb", 2, [128, 4, 32], BF16)
                obuf = Ring(A, "obuf", 2, [128, 512], F32)
                obt = Ring(A, "obt", 2, [128, 512], F32)
                osum = A("osum", [128, 512], F32)
                sq4 = A("sq4", [128, 128], F32)
                ss4 = A("ss4", [128, 8], F32)
                ghb = Ring(A, "ghb", 2, [128, 512], BF16)
                gg = A("gg", [128, 512], F32)
                og = A("og", [128, 512], BF16)
                ogT = A("ogT", [128, 4, 128], BF16)
                odT = Ring(A, "odT", 2, [128, 4, 128], BF16)
                mgh = Ring(A, "mgh", 2, [128, D], BF16)
                mgd = Ring(A, "mgd", 2, [128, D], BF16)
                m1 = A("m1", [128, D], F32)
                m2 = A("m2", [128, D], F32)
                mb = A("mb", [128, D], BF16)
                mT = A("mT", [128, 8, 128], BF16)
                junk2 = A("junk2", [128, D], F32)
                xr = Ring(A, "xr", 2, [128, D], F32)
                yt = Ring(A, "yt", 2, [128, D], F32)
                pb = [PS(f"pb{i}", [128, 512], F32) for i in range(3)]
                pA = PS("pA", [128, 4, 128], F32)
                po = PS("po", [128, 4, 128], F32)
                pSt = PS("pSt", [128, 4, 128], F32)
                pTr = PS("pTr", [128, 8, 128], BF16)
                pdec = PS("pdec", [128, 512], F32)

                def hg_tile(sg, so, t, d):
                    tok = slice(so + t * 128, so + (t + 1) * 128)
                    q_, qk_ = qh.next()
                    fw.dma("sp", q_[:], s_q[tok, :], writes=[qk_])
                    f_, fk_ = fin.next()
                    fw.dma("sp", f_[:], s_f[d][tok, :], writes=[fk_])
                    v_, vk_ = vh.next()
                    fw.dma("sp", v_[:], s_v[tok, :], writes=[vk_])
                    g_, gk_ = gl.next()
                    fw.op("act", lambda: S_.activation(out=g_[:], in_=f_[:], func=AF.Ln), reads=[fk_], writes=[gk_])
                    k_, kk_ = kk.next()
                    fw.op("dve", lambda: V.tensor_scalar(k_[:], f_[:], -1.0, 1.0, ALU.mult, ALU.add), reads=[fk_], writes=[kk_])
                    for i in range(3):
                        fw.op("pe", lambda: T.matmul(pb[i][:], lhsT=mats[:, 3 * d + i, :], rhs=g_[:], start=True, stop=True), reads=[gk_, "mats"], writes=[f"pb{i}"])

                    def dmm():
                        for h in range(4):
                            i_ = T.matmul(pdec[:, 4 * h:4 * h + 4], lhsT=g_[:, h * 128:(h + 1) * 128], rhs=ind[:], start=True, stop=True)
                        return i_
                    fw.op("pe", dmm, reads=[gk_, "ind"], writes=["pdec"])
                    es = [ex.next() for _ in range(4)]
                    pc_, pck_ = pbc.next()
                    fw.op("dve", lambda: V.tensor_scalar(pc_[:], pb[0][:], 43.0, -43.0, ALU.min, ALU.max), reads=["pb0"], writes=[pck_])
                    fw.op("act", lambda: S_.activation(out=es[0][0][:], in_=pc_[:], func=AF.Exp), reads=[pck_], writes=[es[0][1]])
                    fw.op("act", lambda: S_.activation(out=es[1][0][:], in_=pc_[:], func=AF.Exp, scale=-1.0), reads=[pck_], writes=[es[1][1]])
                    fw.op("act", lambda: S_.activation(out=es[2][0][:], in_=pb[1][:], func=AF.Exp), reads=["pb1"], writes=[es[2][1]])
                    fw.op("act", lambda: S_.activation(out=es[3][0][:], in_=pb[2][:], func=AF.Exp), reads=["pb2"], writes=[es[3][1]])
                    dc, dck = dec.next()
                    fw.op("act", lambda: S_.activation(out=dc[:], in_=pdec[:, 0:16], func=AF.Exp), reads=["pdec"], writes=[dck])
                    a_, ak_ = qt_.next()
                    fw.op("dve", lambda: V.tensor_tensor(a_[:], q_[:], es[0][0][:], ALU.mult), reads=[qk_, es[0][1]], writes=[ak_])
                    b_, bk_ = kt_.next()
                    fw.op("dve", lambda: V.tensor_tensor(b_[:], k_[:], es[1][0][:], ALU.mult), reads=[kk_, es[1][1]], writes=[bk_])
                    c_, ck_ = qi_.next()
                    fw.op("dve", lambda: V.tensor_tensor(c_[:], q_[:], es[2][0][:], ALU.mult), reads=[qk_, es[2][1]], writes=[ck_])
                    e_, ek_ = kh_.next()
                    fw.op("dve", lambda: V.tensor_tensor(e_[:], k_[:], es[3][0][:], ALU.mult), reads=[kk_, es[3][1]], writes=[ek_])
                    tts = []
                    for src, srck in ((a_, ak_), (b_, bk_), (c_, ck_)):
                        def tr():
                            for h in range(4):
                                i_ = T.transpose(pTr[:, h, :], src[:, h * 128:(h + 1) * 128], identb[:])
                            return i_
                        fw.op("pe", tr, reads=[srck, "identb"], writes=["pTr"])
                        tt, tk = trs.next()
                        fw.op("act", lambda: S_.copy(tt[:], pTr[:, 0:4, :]), reads=["pTr"], writes=[tk])
                        tts.append((tt, tk))
                    (qtT, qtTk), (ktT, ktTk), (qiT, qiTk) = tts
                    St, Sb_ = Sst[d], Sbf[d]
                    for c in ((0, 1, 2, 3) if d == 0 else (3, 2, 1, 0)):
                        rc = slice(32 * c, 32 * c + 32)

                        def sc():
                            for h in range(4):
                                i_ = T.matmul(pA[rc, h, 0:32], lhsT=ktT[:, h, rc], rhs=qtT[:, h, rc], start=True, stop=True, tile_position=(0, 32 * c))
                            return i_
                        fw.op("pe", sc, reads=[qtTk, ktTk], writes=["pA"])
                        at, atk = Asb.next()
                        fw.op("dve", lambda: V.tensor_tensor(at[rc, :, :], pA[rc, :, 0:32], masks[rc, d, :].rearrange("p (h t) -> p h t", h=4), ALU.mult), reads=["pA", "masks"], writes=[atk])

                        def om_():
                            for h in range(4):
                                T.matmul(po[rc, h, :], lhsT=at[rc, h, :], rhs=v_[rc, h * 128:(h + 1) * 128], start=True, stop=False, tile_position=(32 * c, 32 * c))
                                i_ = T.matmul(po[rc, h, :], lhsT=qiT[:, h, rc], rhs=Sb_[:, h, :], start=False, stop=True, tile_position=(0, 32 * c))
                            return i_
                        fw.op("pe", om_, reads=[atk, vk_, qiTk, f"Sbf{d}"], writes=["po"])

                        def sm_():
                            for h in range(4):
                                i_ = T.matmul(pSt[:, h, :], lhsT=e_[rc, h * 128:(h + 1) * 128], rhs=v_[rc, h * 128:(h + 1) * 128], start=True, stop=True, tile_position=(32 * c, 0))
                            return i_
                        fw.op("pe", sm_, reads=[ek_, vk_], writes=["pSt"])

                        def su_():
                            for h in range(4):
                                i_ = V.scalar_tensor_tensor(St[:, h, :], St[:, h, :], dc[:, 4 * h + c:4 * h + c + 1], pSt[:, h, :], ALU.mult, ALU.add)
                            return i_
                        fw.op("dve", su_, reads=["pSt", dck, f"S{d}"], writes=[f"S{d}"])
                        fw.op("act", lambda: S_.copy(Sb_[:], St[:]), reads=[f"S{d}"], writes=[f"Sbf{d}"])

                for sg, L in enumerate(segs):
                    so = offs[sg]
                    nt = L // 128
                    r0 = (l * nseg + sg) * 3
                    fw.dma("sp", GPb[:], s_mod[r0 + 2:r0 + 3, :].partition_broadcast(128), writes=["GPb"])
                    for d in range(2):
                        fw.op("dve", lambda: V.memset(Sst[d][:], 0.0), writes=[f"S{d}"])
                        fw.op("dve", lambda: V.memset(Sbf[d][:], 0.0), writes=[f"Sbf{d}"])
                    for t in reversed(range(nt)):
                        tok = slice(so + t * 128, so + (t + 1) * 128)
                        hg_tile(sg, so, t, 1)
                        ob_, obk_ = obuf.next()
                        fw.op("act", lambda: S_.copy(ob_[:], po[:].rearrange("p h v -> p (h v)")), reads=["po"], writes=[obk_])
                        fw.dma("pool", s_ob[tok, :], ob_[:], reads=[obk_], writes=[f"sob{so // 128 + t}"])
                    for t in range(nt):
                        tok = slice(so + t * 128, so + (t + 1) * 128)
                        hg_tile(sg, so, t, 0)
                        obl, oblk = obt.next()
                        fw.dma("sp", obl[:], s_ob[tok, :], reads=[f"sob{so // 128 + t}"], writes=[oblk])
                        fw.op("dve", lambda: V.tensor_tensor(osum[:], po[:].rearrange("p h v -> p (h v)"), obl[:], ALU.add), reads=["po", oblk], writes=["osum"])
                        fw.op("dve", lambda: V.memset(ss4[:, 0:4], 0.0), writes=["ss4"])

                        def sqs():
                            for h in range(4):
                                i_ = S_.activation(out=sq4[:], in_=osum[:, h * 128:(h + 1) * 128], func=AF.Square, accum_out=ss4[:, h:h + 1])
                            return i_
                        fw.op("act", sqs, reads=["osum", "ss4"], writes=["sq4", "ss4"])
                        fw.op("dve", lambda: V.tensor_scalar(ss4[:, 4:8], ss4[:, 0:4], 1.0 / 128, NORM_EPS, ALU.mult, ALU.add), reads=["ss4"], writes=["ss4b"])
                        fw.op("act", lambda: S_.activation(out=ss4[:, 4:8], in_=ss4[:, 4:8], func=AF.Ln), reads=["ss4b"], writes=["ss4b"])
                        fw.op("act", lambda: S_.activation(out=ss4[:, 4:8], in_=ss4[:, 4:8], func=AF.Exp, scale=-0.5), reads=["ss4b"], writes=["ss4b"])
                        gh_, ghk_ = ghb.next()
                        fw.dma("sp", gh_[:], s_gh[tok, :], writes=[ghk_])
                        fw.op("dve", lambda: V.tensor_tensor(gg[:], gh_[:], hgnb[:, l, :], ALU.mult), reads=[ghk_] + [f"hgnb{l}{h}" for h in range(4)], writes=["gg"])

                        def ogf():
                            for h in range(4):
                                i_ = V.scalar_tensor_tensor(og[:, h * 128:(h + 1) * 128], osum[:, h * 128:(h + 1) * 128], ss4[:, 4 + h:5 + h], gg[:, h * 128:(h + 1) * 128], ALU.mult, ALU.mult)
                            return i_
                        fw.op("dve", ogf, reads=["osum", "ss4b", "gg"], writes=["og"])

                        def tr():
                            for h in range(4):
                                i_ = T.transpose(pTr[:, h, :], og[:, h * 128:(h + 1) * 128], identb[:])
                            return i_
                        fw.op("pe", tr, reads=["og", "identb"], writes=["pTr"])
                        fw.op("act", lambda: S_.copy(ogT[:], pTr[:, 0:4, :]), reads=["pTr"], writes=["ogT"])
                        od_, odk_ = odT.next()
                        fw.dma("sp", od_[:], s_OT[:, :, tok].rearrange("h p t -> p h t"), writes=[odk_])
                        mh_, mhk_ = mgh.next()
                        fw.dma("sp", mh_[:], s_mgh[tok, :], writes=[mhk_])
                        md_, mdk_ = mgd.next()
                        fw.dma("sp", md_[:], s_mgd[tok, :], writes=[mdk_])
                        for (lt, ltk, W_, Wk_, mg_, mgk_, mo, mok) in ((ogT, "ogT", Wph, "Wph", mh_, mhk_, m1, "m1"), (od_, odk_, Wpd, "Wpd", md_, mdk_, m2, "m2")):
                            def pm():
                                for n in range(2):
                                    for h in range(4):
                                        i_ = T.matmul(pb[n][:], lhsT=lt[:, h, :], rhs=W_[:, h, n * 512:(n + 1) * 512], start=(h == 0), stop=(h == 3))
                                return i_
                            fw.op("pe", pm, reads=[ltk, Wk_], writes=["pb0", "pb1"])

                            def gm():
                                for n in range(2):
                                    i_ = V.tensor_tensor(mo[:, n * 512:(n + 1) * 512], pb[n][:], mg_[:, n * 512:(n + 1) * 512], ALU.mult)
                                return i_
                            fw.op("dve", gm, reads=["pb0", "pb1", mgk_], writes=[mok])
                        fw.op("dve", lambda: V.tensor_tensor(mb[:], m1[:], m2[:], ALU.add), reads=["m1", "m2"], writes=["mb"])

                        def tr8():
                            for k in range(8):
                                i_ = T.transpose(pTr[:, k, :], mb[:, k * 128:(k + 1) * 128], identb[:])
                            return i_
                        fw.op("pe", tr8, reads=["mb", "identb"], writes=["pTr"])
                        fw.op("act", lambda: S_.copy(mT[:], pTr[:]), reads=["pTr"], writes=["mT"])

                        def om2():
                            for n in range(2):
                                for k in range(8):
                                    i_ = T.matmul(pb[n][:], lhsT=mT[:, k, :], rhs=Wo[:, k, n * 512:(n + 1) * 512], start=(k == 0), stop=(k == 7))
                            return i_
                        fw.op("pe", om2, reads=["mT", "Wo"], writes=["pb0", "pb1"])
                        fw.op("dve", lambda: V.memset(ss4[:, 0:2], 0.0), writes=["ss4"])

                        def sq2():
                            for n in range(2):
                                i_ = S_.activation(out=junk2[:, n * 512:(n + 1) * 512], in_=pb[n][:], func=AF.Square, accum_out=ss4[:, n:n + 1])
                            return i_
                        fw.op("act", sq2, reads=["pb0", "pb1", "ss4"], writes=["junk2", "ss4"])
                        fw.op("dve", lambda: V.tensor_tensor(ss4[:, 2:3], ss4[:, 0:1], ss4[:, 1:2], ALU.add), reads=["ss4"], writes=["ss4c"])
                        fw.op("dve", lambda: V.tensor_scalar(ss4[:, 2:3], ss4[:, 2:3], 1.0 / D, NORM_EPS, ALU.mult, ALU.add), reads=["ss4c"], writes=["ss4c"])
                        fw.op("act", lambda: S_.activation(out=ss4[:, 2:3], in_=ss4[:, 2:3], func=AF.Ln), reads=["ss4c"], writes=["ss4c"])
                        fw.op("act", lambda: S_.activation(out=ss4[:, 3:4], in_=ss4[:, 2:3], func=AF.Exp, scale=-0.5), reads=["ss4c"], writes=["ss4d"])
                        x_, xk_ = xr.next()
                        fw.dma("sp", x_[:], xsrc[tok, :], writes=[xk_])
                        y_, yk_ = yt.next()

                        def yo():
                            for n in range(2):
                                i_ = V.scalar_tensor_tensor(y_[:, n * 512:(n + 1) * 512], pb[n][:], ss4[:, 3:4], GPb[:, n * 512:(n + 1) * 512], ALU.mult, ALU.mult)
                            return i_
                        fw.op("dve", yo, reads=["pb0", "pb1", "ss4d", "GPb"], writes=[yk_])
                        fw.op("dve", lambda: V.tensor_tensor(y_[:], y_[:], x_[:], ALU.add), reads=[yk_, xk_], writes=[yk_])
                        fw.dma("pool", ydst[tok, :], y_[:], reads=[yk_])
                fw.barrier()
    return nc


_CACHE = {}


def run(segs, x_cores, c_cores, weights, n_cores=N_CORES, nl=DEPTH, debug=False):
    key = (tuple(segs), nl, debug)
    if key not in _CACHE:
        _CACHE[key] = (build(list(segs), nl, debug), host_consts(max(segs) // 128))
    nc, kc = _CACHE[key]
    f = lambda a: np.ascontiguousarray(np.asarray(a, dtype=np.float32))
    common = {
        "w_ada": f(weights["w_ada"]), "b_ada": f(weights["b_ada"]), "g_pre": f(weights["g_pre"]), "g_post": f(weights["g_post"]),
        "w_in": f(weights["w_in"]), "hg_lb": f(weights["hg_lower_bounds"]).reshape(1, -1), "hg_norm_g": f(weights["hg_norm_g"]),
        "da_lambda": f(weights["da_lambda"]).reshape(DEPTH, 256), "da_subln_g": f(weights["da_subln_g"]),
        "w_proj_hg": f(weights["w_proj_hg"]), "w_proj_da": f(weights["w_proj_da"]), "w_out": f(weights["w_out"]),
        "k_ident": kc["ident"], "k_mats": kc["mats"], "k_ind": kc["ind"], "k_masks": kc["masks"], "k_rope": kc["rope"],
    }
    in_maps = []
    for i in range(n_cores):
        c = f(c_cores[i])
        c_lay = np.ascontiguousarray(c.reshape(len(segs), 8, 128).transpose(0, 2, 1))
        in_maps.append({"x": f(x_cores[i]), "c": c_lay, **common})
    res = run_bass_kernel_spmd(nc, in_maps, core_ids=list(range(n_cores)))
    if debug:
        return res.results
    return [r["y"] for r in res.results]


def kernel(x_prompt, x_sample, c_prompt, c_sample, w_ada, b_ada, g_pre, g_post, w_in, hg_lower_bounds, hg_norm_g,
           da_lambda, da_subln_g, w_proj_hg, w_proj_da, w_out):
    x_prompt = np.asarray(x_prompt, np.float32)
    x_sample = np.asarray(x_sample, np.float32)
    c_prompt = np.asarray(c_prompt, np.float32)
    c_sample = np.asarray(c_sample, np.float32)
    B, L, _ = x_prompt.shape
    Bs, Ls, _ = x_sample.shape
    per = B // N_CORES
    segs = [L] * per + [Ls]
    weights = dict(w_ada=w_ada, b_ada=b_ada, g_pre=g_pre, g_post=g_post, w_in=w_in, hg_lower_bounds=hg_lower_bounds,
                   hg_norm_g=hg_norm_g, da_lambda=da_lambda, da_subln_g=da_subln_g, w_proj_hg=w_proj_hg,
                   w_proj_da=w_proj_da, w_out=w_out)
    xc, cc = [], []
    for i in range(N_CORES):
        xs = [x_prompt[i * per + j] for j in range(per)] + [x_sample[i % Bs]]
        cs = [c_prompt[i * per + j] for j in range(per)] + [c_sample[i % Bs]]
        xc.append(np.concatenate(xs, axis=0))
        cc.append(np.stack(cs, axis=0))
    ys = run(segs, xc, cc, weights)
    y_prompt = np.empty_like(x_prompt)
    y_sample = np.empty_like(x_sample)
    for i in range(N_CORES):
        for j in range(per):
            y_prompt[i * per + j] = ys[i][j * L:(j + 1) * L]
    for s in range(Bs):
        y_sample[s] = ys[s][per * L: per * L + Ls]
    return (y_prompt, y_sample)
```

```python
import contextlib
import math
import numpy as np
import concourse.bass as bass
import concourse.mybir as mybir
from concourse.bass_utils import run_bass_kernel_spmd

F32 = mybir.dt.float32
BF16 = mybir.dt.bfloat16
AF = mybir.ActivationFunctionType
ALU = mybir.AluOpType
AX = mybir.AxisListType

D = 1024
DIN = 6656
DEPTH = 2
NORM_EPS = 1e-6
SUBLN_EPS = 1e-5
ROPE_THETA = 500000.0
N_CORES = 8


class FW:
    def __init__(self, nc, n_dma_sems=48):
        self.nc = nc
        self.engs = {"pe": nc.tensor, "act": nc.scalar, "dve": nc.vector, "pool": nc.gpsimd, "sp": nc.sync}
        self.sem = {k: nc.alloc_semaphore(name="s_" + k) for k in ("pe", "act", "dve", "pool")}
        self.cnt = {k: 0 for k in self.sem}
        self.dma_sems = [nc.alloc_semaphore(name=f"s_dma{i}") for i in range(n_dma_sems)]
        self.dma_cnt = [0] * n_dma_sems
        self.dma_rr = 0
        self.seen = {k: {} for k in self.engs}
        self.lastw = {}
        self.readers = {}

    def _wait(self, eng, tok):
        if tok is None:
            return
        key, sem, val = tok
        s = self.seen[eng]
        if s.get(key, 0) >= val:
            return
        self.engs[eng].wait_ge(sem, val)
        s[key] = val

    def _deps(self, eng, reads, writes):
        for r in reads:
            self._wait(eng, self.lastw.get(r))
        for w in writes:
            self._wait(eng, self.lastw.get(w))
            for k, tok in self.readers.get(w, {}).items():
                if k == eng:
                    continue
                self._wait(eng, tok)

    def _commit(self, tok, reads, writes):
        for w in writes:
            self.lastw[w] = tok
            self.readers[w] = {}
        for r in reads:
            if r in writes:
                continue
            self.readers.setdefault(r, {})[tok[0]] = tok

    def op(self, eng, fn, reads=(), writes=()):
        self._deps(eng, reads, writes)
        ins = fn()
        self.cnt[eng] += 1
        ins.then_inc(self.sem[eng], 1)
        tok = (eng, self.sem[eng], self.cnt[eng])
        self._commit(tok, reads, writes)
        return tok

    def dma(self, q, out, in_, reads=(), writes=(), **kw):
        i = self.dma_rr
        self.dma_rr = (self.dma_rr + 1) % len(self.dma_sems)
        sem = self.dma_sems[i]
        key = f"dma{i}"
        if self.dma_cnt[i] > 0:
            self._wait(q, (key, sem, self.dma_cnt[i]))
        self._deps(q, reads, writes)
        self.dma_cnt[i] += 16
        self.engs[q].dma_start(out=out, in_=in_, **kw).then_inc(sem, 16)
        tok = (key, sem, self.dma_cnt[i])
        self._commit(tok, reads, writes)
        return tok

    def finish(self, q):
        for i, sem in enumerate(self.dma_sems):
            if self.dma_cnt[i] > 0:
                self._wait(q, (f"dma{i}", sem, self.dma_cnt[i]))
        for k in self.sem:
            if self.cnt[k] > 0:
                self._wait(q, (k, self.sem[k], self.cnt[k]))

    def barrier(self):
        for q in self.engs:
            self.finish(q)


class Ring:
    def __init__(self, alloc, name, n, shape, dt):
        self.t = [alloc(f"{name}{i}", shape, dt) for i in range(n)]
        self.i = -1
        self.name = name

    def next(self):
        self.i = (self.i + 1) % len(self.t)
        return self.t[self.i], f"{self.name}{self.i}"


def lambda_init(l):
    return 0.8 - 0.6 * math.exp(-0.3 * l)


def host_consts(tmax):
    ident = np.eye(128, dtype=np.float32)
    idx = np.arange(128)
    ch = idx // 32
    loc = idx % 32
    same = (ch[:, None] == ch[None, :])
    t_, s_ = idx[:, None], idx[None, :]
    mats = np.zeros((6, 128, 128), np.float32)
    MQf = (same & (s_ <= t_)).astype(np.float32) - (same & (loc[None, :] <= 15)).astype(np.float32)
    MLf = (same & (s_ <= t_)).astype(np.float32)
    MKf = (same & (s_ > t_)).astype(np.float32)
    MQb = (same & (s_ >= t_)).astype(np.float32) - (same & (loc[None, :] >= 16)).astype(np.float32)
    MLb = (same & (s_ >= t_)).astype(np.float32)
    MKb = (same & (s_ < t_)).astype(np.float32)
    for i, m in enumerate((MQf, MLf, MKf, MQb, MLb, MKb)):
        mats[i] = m.T
    ind = np.zeros((128, 4), np.float32)
    ind[idx, ch] = 1.0
    masks = np.zeros((2, 128, 4, 32), np.float32)
    tl = np.arange(32)
    masks[0] = (tl[None, :] >= loc[:, None]).astype(np.float32)[:, None, :]
    masks[1] = (tl[None, :] <= loc[:, None]).astype(np.float32)[:, None, :]
    rope_dim = 16
    inv_freq = np.power(np.float32(ROPE_THETA), -np.arange(0, rope_dim, 2, dtype=np.float32) / np.float32(rope_dim)).astype(np.float32)
    pos = np.arange(tmax * 128, dtype=np.float32)
    ang = (pos[:, None] * inv_freq[None, :]).astype(np.float32)
    cos = np.cos(ang).astype(np.float32)
    sin = np.sin(ang).astype(np.float32)
    rope = np.zeros((tmax, 128, 128), np.float32)
    rope[:, :, 0:64] = np.tile(cos, (1, 8)).reshape(tmax, 128, 64)
    rope[:, :, 64:128] = np.tile(sin, (1, 8)).reshape(tmax, 128, 64)
    return dict(ident=ident, mats=mats, ind=ind, masks=masks.reshape(2, 128, 128), rope=rope)


def build(segs, nl=DEPTH, debug=False):
    nseg = len(segs)
    offs = [0]
    for L in segs:
        assert L % 512 == 0
        offs.append(offs[-1] + L)
    Ltot = offs[-1]
    tmax = max(segs) // 128
    nc = bass.Bass("TRN2", target_bir_lowering=False)

    def din(name, shape, dt=F32):
        return nc.dram_tensor(name, shape, dt, kind="ExternalInput").ap()

    def dscr(name, shape, dt):
        return nc.dram_tensor(name, shape, dt, kind="ExternalOutput" if debug else "Internal").ap()

    x_in = din("x", [Ltot, D])
    c_in = din("c", [nseg, 128, 8])
    w_ada = din("w_ada", [DEPTH, D, 3 * D])
    b_ada = din("b_ada", [DEPTH, 3 * D])
    g_pre = din("g_pre", [DEPTH, D])
    g_post = din("g_post", [DEPTH, D])
    w_in = din("w_in", [DEPTH, D, DIN])
    hg_lb = din("hg_lb", [1, DEPTH * 1024])
    hg_norm_g = din("hg_norm_g", [DEPTH, 128])
    da_lambda = din("da_lambda", [DEPTH, 256])
    da_subln_g = din("da_subln_g", [DEPTH, 128])
    w_proj_hg = din("w_proj_hg", [DEPTH, 512, D])
    w_proj_da = din("w_proj_da", [DEPTH, 512, D])
    w_out = din("w_out", [DEPTH, D, D])
    k_ident = din("k_ident", [128, 128])
    k_mats = din("k_mats", [6, 128, 128])
    k_ind = din("k_ind", [128, 4])
    k_masks = din("k_masks", [2, 128, 128])
    k_rope = din("k_rope", [tmax, 128, 128])
    y_out = nc.dram_tensor("y", [Ltot, D], F32, kind="ExternalOutput").ap()

    s_q = dscr("s_q", [Ltot, 512], BF16)
    s_f = [dscr("s_ff", [Ltot, 512], F32), dscr("s_fb", [Ltot, 512], F32)]
    s_v = dscr("s_v", [Ltot, 512], BF16)
    s_gh = dscr("s_gh", [Ltot, 512], BF16)
    s_QT = dscr("s_QT", [4, 128, Ltot], BF16)
    s_KT = dscr("s_KT", [4, 128, Ltot], BF16)
    s_GT = dscr("s_GT", [4, 128, Ltot], BF16)
    s_OT = dscr("s_OT", [4, 128, Ltot], BF16)
    s_V = dscr("s_V", [Ltot, 512], BF16)
    s_mgh = dscr("s_mgh", [Ltot, D], BF16)
    s_mgd = dscr("s_mgd", [Ltot, D], BF16)
    s_ob = dscr("s_ob", [Ltot, 512], F32)
    s_y0 = dscr("s_y0", [Ltot, D], F32)
    s_mod = dscr("s_mod", [DEPTH * nseg * 3, D], F32)

    fw = FW(nc)
    V, S_, P_, T = nc.vector, nc.scalar, nc.gpsimd, nc.tensor

    with contextlib.ExitStack() as top:
        def SB(name, shape, dt):
            return top.enter_context(nc.sbuf_tensor(name, shape, dt))

        identf = SB("identf", [128, 128], F32)
        identb = SB("identb", [128, 128], BF16)
        onesb = SB("onesb", [128, 128], BF16)
        mats = SB("mats", [128, 6, 128], F32)
        ind = SB("ind", [128, 4], F32)
        masks = SB("masks", [128, 2, 128], F32)
        LB = SB("LB", [128, DEPTH, 1024], F32)
        OMLB = SB("OMLB", [128, DEPTH, 1024], F32)
        neglam = SB("neglam", [128, DEPTH], F32)
        hgnb = SB("hgnb", [128, DEPTH, 512], F32)
        gsub = SB("gsub", [128, DEPTH], F32)

        with contextlib.ExitStack() as ph:
            def A(name, shape, dt):
                return ph.enter_context(nc.sbuf_tensor(name, shape, dt))

            fw.dma("sp", identf[:], k_ident[:, :], writes=["identf"])
            fw.dma("pool", identb[:], k_ident[:, :], writes=["identb"])
            fw.op("dve", lambda: V.memset(onesb[:], 1.0), writes=["onesb"])
            fw.dma("sp", mats[:], k_mats.rearrange("m p t -> p m t"), writes=["mats"])
            fw.dma("sp", ind[:], k_ind[:, :], writes=["ind"])
            fw.dma("sp", masks[:], k_masks.rearrange("d p c -> p d c"), writes=["masks"])
            lbraw = A("lbraw", [128, DEPTH * 1024], F32)
            lbe = A("lbe", [128, DEPTH * 1024], F32)
            lbt = A("lbt", [128, 4, 1024], F32)
            fw.dma("sp", lbraw[:], hg_lb[0:1, :].partition_broadcast(128), writes=["lbraw"])
            fw.op("act", lambda: S_.activation(out=lbe[:], in_=lbraw[:], func=AF.Exp), reads=["lbraw"], writes=["lbe"])
            fw.op("dve", lambda: V.tensor_tensor(lbt[:, 0, :], lbe[:, 0:1024], lbe[:, 1024:2048], ALU.add), reads=["lbe"], writes=["lbt0"])
            fw.op("dve", lambda: V.reciprocal(lbt[:, 1, :], lbt[:, 0, :]), reads=["lbt0"], writes=["lbt1"])
            fw.op("dve", lambda: V.tensor_tensor(lbt[:, 2, :], lbe[:, 0:1024], lbt[:, 1, :], ALU.mult), reads=["lbe", "lbt1"], writes=["lbt2"])
            fw.op("dve", lambda: V.tensor_tensor(lbt[:, 3, :], lbe[:, 1024:2048], lbt[:, 1, :], ALU.mult), reads=["lbe", "lbt1"], writes=["lbt3"])
            fw.op("dve", lambda: V.tensor_tensor(lbt[:, 0, :], lbt[:, 2, :], lbt[:, 3, :], ALU.add), reads=["lbt2", "lbt3", "lbt1"], writes=["lbt0"])
            fw.op("dve", lambda: V.tensor_tensor(LB[:, 0, :], lbt[:, 2, :], lbt[:, 2, :], ALU.subtract), reads=["lbt2"], writes=["LB0"])
            fw.op("dve", lambda: V.tensor_tensor(LB[:, 1, :], lbt[:, 0, :], lbt[:, 2, :], ALU.subtract), reads=["lbt0", "lbt2"], writes=["LB1"])
            fw.op("dve", lambda: V.tensor_scalar(OMLB[:], LB[:], -1.0, 1.0, ALU.mult, ALU.add), reads=["LB0", "LB1"], writes=["OMLB"])
            lamr = A("lamr", [128, DEPTH, 256], F32)
            lamp = A("lamp", [128, DEPTH, 2, 64], F32)
            lams = A("lams", [128, DEPTH, 4], F32)
            sgr = A("sgr", [128, DEPTH], F32)
            for l in range(DEPTH):
                fw.dma("sp", lamr[:, l, :], da_lambda[l:l + 1, :].partition_broadcast(128), writes=[f"lamr{l}"])
                fw.op("dve", lambda: V.tensor_tensor(lamp[:, l, 0, :], lamr[:, l, 0:64], lamr[:, l, 64:128], ALU.mult), reads=[f"lamr{l}"], writes=[f"lamp{l}a"])
                fw.op("dve", lambda: V.tensor_tensor(lamp[:, l, 1, :], lamr[:, l, 128:192], lamr[:, l, 192:256], ALU.mult), reads=[f"lamr{l}"], writes=[f"lamp{l}b"])
                fw.op("dve", lambda: V.reduce_sum(lams[:, l, 0:2], lamp[:, l, :, :], AX.X), reads=[f"lamp{l}a", f"lamp{l}b"], writes=[f"lams{l}"])
                fw.op("act", lambda: S_.activation(out=lams[:, l, 2:4], in_=lams[:, l, 0:2], func=AF.Exp), reads=[f"lams{l}"], writes=[f"lame{l}"])
                fw.op("dve", lambda: V.tensor_tensor(lams[:, l, 0:1], lams[:, l, 3:4], lams[:, l, 2:3], ALU.subtract), reads=[f"lame{l}"], writes=[f"lams{l}"])
                fw.op("dve", lambda: V.tensor_scalar(neglam[:, l:l + 1], lams[:, l, 0:1], -lambda_init(l), None, ALU.add), reads=[f"lams{l}"], writes=["neglam"])
                for h in range(4):
                    fw.dma("sp", hgnb[:, l, h * 128:(h + 1) * 128], hg_norm_g[l:l + 1, :].partition_broadcast(128), writes=[f"hgnb{l}{h}"])
                fw.dma("sp", sgr[:, l:l + 1], da_subln_g[l, :].rearrange("(p o) -> p o", o=1), writes=[f"sgr{l}"])
                fw.op("dve", lambda: V.tensor_scalar(gsub[:, l:l + 1], sgr[:, l:l + 1], 1.0 - lambda_init(l), None, ALU.mult), reads=[f"sgr{l}"], writes=["gsub"])

            wada = A("wada", [128, 8, 3 * D], BF16)
            badab = A("badab", [128, 3 * D], F32)
            gpreb = A("gpreb", [128, D], F32)
            gpostb = A("gpostb", [128, D], F32)
            cs = A("cs", [128, 8], F32)
            css = A("css", [128, 8], F32)
            cb = A("cb", [128, 8, 128], BF16)
            modt = A("modt", [128, 3 * D], F32)
            Gt = A("Gt", [128, D], F32)
            GPt = A("GPt", [128, D], F32)
            with nc.psum_tensor("pzm", [128, 512], F32) as pzm:
                for l in range(DEPTH):
                    for k in range(8):
                        fw.dma("pool", wada[:, k, :], w_ada[l, k * 128:(k + 1) * 128, :], writes=["wada"])
                    fw.dma("sp", badab[:], b_ada[l:l + 1, :].partition_broadcast(128), writes=["badab"])
                    fw.dma("sp", gpreb[:], g_pre[l:l + 1, :].partition_broadcast(128), writes=["gpreb"])
                    fw.dma("sp", gpostb[:], g_post[l:l + 1, :].partition_broadcast(128), writes=["gpostb"])
                    for sg in range(nseg):
                        fw.dma("sp", cs[:], c_in[sg, :, :], writes=["cs"])
                        fw.op("act", lambda: S_.activation(out=css[:], in_=cs[:], func=AF.Silu), reads=["cs"], writes=["css"])

                        def bc():
                            for k in range(8):
                                i = V.tensor_scalar(cb[:, k, :], onesb[:], css[:, k:k + 1], None, ALU.mult)
                            return i
                        fw.op("dve", bc, reads=["css", "onesb"], writes=["cb"])
                        for g in range(6):
                            def mm():
                                for k in range(8):
                                    i = T.matmul(pzm[:], lhsT=cb[:, k, :], rhs=wada[:, k, g * 512:(g + 1) * 512], start=(k == 0), stop=(k == 7))
                                return i
                            fw.op("pe", mm, reads=["cb", "wada"], writes=["pzm"])
                            fw.op("dve", lambda: V.tensor_tensor(modt[:, g * 512:(g + 1) * 512], pzm[:], badab[:, g * 512:(g + 1) * 512], ALU.add), reads=["pzm", "badab"], writes=["modt"])
                        fw.op("dve", lambda: V.scalar_tensor_tensor(Gt[:], modt[:, D:2 * D], 1.0, gpreb[:], ALU.add, ALU.mult), reads=["modt", "gpreb"], writes=["Gt"])
                        fw.op("dve", lambda: V.tensor_tensor(GPt[:], modt[:, 2 * D:3 * D], gpostb[:], ALU.mult), reads=["modt", "gpostb"], writes=["GPt"])
                        r0 = (l * nseg + sg) * 3
                        fw.dma("pool", s_mod[r0:r0 + 1, :], Gt[0:1, :], reads=["Gt"])
                        fw.dma("pool", s_mod[r0 + 1:r0 + 2, :], modt[0:1, 0:D], reads=["modt"])
                        fw.dma("pool", s_mod[r0 + 2:r0 + 3, :], GPt[0:1, :], reads=["GPt"])
            fw.barrier()

        for l in range(nl):
            xsrc = x_in if l == 0 else s_y0
            ydst = y_out if l == nl - 1 else s_y0

            with contextlib.ExitStack() as ph:
                def A(name, shape, dt):
                    return ph.enter_context(nc.sbuf_tensor(f"{name}_a{l}", shape, dt))

                def PS(name, shape, dt):
                    return ph.enter_context(nc.psum_tensor(f"{name}_a{l}", shape, dt))

                Win = A("Win", [128, 8, DIN], BF16)
                for k in range(8):
                    fw.dma("pool", Win[:, k, :], w_in[l, k * 128:(k + 1) * 128, :], writes=["Win"])
                Gb = A("Gb", [128, D], F32)
                SHb = A("SHb", [128, D], F32)
                xs = Ring(A, "xs", 2, [128, D], F32)
                junk = A("junk", [128, D], F32)
                ssq = A("ssq", [128, tmax], F32)
                rstd = A("rstd", [128, tmax], F32)
                hn = A("hn", [128, D], F32)
                hb = Ring(A, "hb", 2, [128, D], BF16)
                hT = Ring(A, "hT", 2, [128, 8, 128], BF16)
                rt = Ring(A, "rt", 2, [128, 128], F32)
                evf = Ring(A, "evf", 3, [128, 512], F32)
                evb = Ring(A, "evb", 6, [128, 512], BF16)
                rtmp = A("rtmp", [128, 4, 64], F32)
                tT = Ring(A, "tT", 2, [128, 4, 128], BF16)
                pT = PS("pT", [128, 8, 128], BF16)
                pz = Ring(PS, "pz", 4, [128, 512], F32)
                pT2 = Ring(PS, "pT2", 2, [128, 8, 128], BF16)

                for sg, L in enumerate(segs):
                    so = offs[sg]
                    nt = L // 128
                    r0 = (l * nseg + sg) * 3
                    fw.dma("sp", Gb[:], s_mod[r0:r0 + 1, :].partition_broadcast(128), writes=["Gb"])
                    fw.dma("sp", SHb[:], s_mod[r0 + 1:r0 + 2, :].partition_broadcast(128), writes=["SHb"])
                    fw.op("dve", lambda: V.memset(ssq[:], 0.0), writes=["ssq"])
                    for t in range(nt):
                        xt, xk = xs.next()
                        fw.dma("sp", xt[:], xsrc[so + t * 128: so + (t + 1) * 128, :], writes=[xk])
                        fw.op("act", lambda: S_.activation(out=junk[:], in_=xt[:], func=AF.Square, accum_out=ssq[:, t:t + 1]), reads=[xk, "ssq"], writes=["junk", "ssq"])
                    fw.op("dve", lambda: V.tensor_scalar(rstd[:, 0:nt], ssq[:, 0:nt], 1.0 / D, NORM_EPS, ALU.mult, ALU.add), reads=["ssq"], writes=["rstd"])
                    fw.op("act", lambda: S_.activation(out=rstd[:, 0:nt], in_=rstd[:, 0:nt], func=AF.Ln), reads=["rstd"], writes=["rstd"])
                    fw.op("act", lambda: S_.activation(out=rstd[:, 0:nt], in_=rstd[:, 0:nt], func=AF.Exp, scale=-0.5), reads=["rstd"], writes=["rstd"])

                    for t in range(nt):
                        tok = slice(so + t * 128, so + (t + 1) * 128)
                        xt, xk = xs.next()
                        fw.dma("sp", xt[:], xsrc[tok, :], writes=[xk])
                        rtt, rk = rt.next()
                        fw.dma("sp", rtt[:], k_rope[t, :, :], writes=[rk])
                        fw.op("dve", lambda: V.scalar_tensor_tensor(hn[:], xt[:], rstd[:, t:t + 1], Gb[:], ALU.mult, ALU.mult), reads=[xk, "rstd", "Gb"], writes=["hn"])
                        hbt, hbk = hb.next()
                        fw.op("dve", lambda: V.tensor_tensor(hbt[:], hn[:], SHb[:], ALU.add), reads=["hn", "SHb"], writes=[hbk])

                        def tr():
                            for k in range(8):
                                i = T.transpose(pT[:, k, :], hbt[:, k * 128:(k + 1) * 128], identb[:])
                            return i
                        fw.op("pe", tr, reads=[hbk, "identb"], writes=["pT"])
                        hTt, hTk = hT.next()
                        fw.op("act", lambda: S_.copy(hTt[:], pT[:]), reads=["pT"], writes=[hTk])

                        for g in range(13):
                            pzt, pzk = pz.next()

                            def mm():
                                for k in range(8):
                                    i = T.matmul(pzt[:], lhsT=hTt[:, k, :], rhs=Win[:, k, g * 512:(g + 1) * 512], start=(k == 0), stop=(k == 7))
                                return i
                            fw.op("pe", mm, reads=[hTk, "Win"], writes=[pzk])
                            if g in (0, 4):
                                eb, ek = evb.next()
                                fw.op("act", lambda: S_.activation(out=eb[:], in_=pzt[:], func=AF.Silu), reads=[pzk], writes=[ek])
                                fw.dma("pool", (s_q if g == 0 else s_gh)[tok, :], eb[:], reads=[ek])
                            elif g in (1, 2):
                                d_ = g - 1
                                ef, efk = evf.next()
                                fw.op("act", lambda: S_.activation(out=ef[:], in_=pzt[:], func=AF.Sigmoid), reads=[pzk], writes=[efk])
                                fw.op("dve", lambda: V.tensor_tensor(ef[:], ef[:], OMLB[:, l, d_ * 512:(d_ + 1) * 512], ALU.mult), reads=[efk, "OMLB"], writes=[efk])
                                fw.op("dve", lambda: V.tensor_tensor(ef[:], ef[:], LB[:, l, d_ * 512:(d_ + 1) * 512], ALU.add), reads=[efk, "LB0", "LB1"], writes=[efk])
                                fw.dma("pool", s_f[d_][tok, :], ef[:], reads=[efk])
                            elif g in (3, 7):
                                eb, ek = evb.next()
                                fw.op("act", lambda: S_.copy(eb[:], pzt[:]), reads=[pzk], writes=[ek])
                                fw.dma("pool", (s_v if g == 3 else s_V)[tok, :], eb[:], reads=[ek])
                            elif g in (5, 6):
                                ef, efk = evf.next()
                                fw.op("act", lambda: S_.copy(ef[:], pzt[:]), reads=[pzk], writes=[efk])
                                eb, ek = evb.next()
                                fw.op("dve", lambda: V.tensor_copy(eb[:], ef[:]), reads=[efk], writes=[ek])
                                v3 = ef[:].rearrange("p (m d) -> p m d", m=8)
                                b3 = eb[:].rearrange("p (m d) -> p m d", m=8)
                                cos3 = rtt[:, 0:64].rearrange("p (m j) -> p m j", m=8)
                                sin3 = rtt[:, 64:128].rearrange("p (m j) -> p m j", m=8)
                                tm = [rtmp[:, i, :].rearrange("p (m j) -> p m j", m=8) for i in range(4)]

                                def rope():
                                    V.tensor_tensor(tm[0], v3[:, :, 0:8], cos3, ALU.mult)
                                    V.tensor_tensor(tm[1], v3[:, :, 8:16], sin3, ALU.mult)
                                    V.tensor_tensor(tm[2], v3[:, :, 8:16], cos3, ALU.mult)
                                    return V.tensor_tensor(tm[3], v3[:, :, 0:8], sin3, ALU.mult)
                                fw.op("dve", rope, reads=[efk, rk], writes=["rtmp"])

                                def rope2():
                                    V.tensor_tensor(b3[:, :, 0:8], tm[0], tm[1], ALU.subtract)
                                    return V.tensor_tensor(b3[:, :, 8:16], tm[2], tm[3], ALU.add)
                                fw.op("dve", rope2, reads=["rtmp", ek], writes=[ek])
                                p2, p2k = pT2.next()

                                def tr2():
                                    for h in range(4):
                                        i = T.transpose(p2[:, h, :], eb[:, h * 128:(h + 1) * 128], identb[:])
                                    return i
                                fw.op("pe", tr2, reads=[ek, "identb"], writes=[p2k])
                                tt, tk = tT.next()
                                fw.op("act", lambda: S_.copy(tt[:], p2[:, 0:4, :]), reads=[p2k], writes=[tk])
                                dst = (s_QT if g == 5 else s_KT)
                                fw.dma("pool", dst[:, :, tok].rearrange("h p t -> p h t"), tt[:], reads=[tk])
                            elif g == 8:
                                eb, ek = evb.next()
                                fw.op("act", lambda: S_.activation(out=eb[:], in_=pzt[:], func=AF.Silu), reads=[pzk], writes=[ek])
                                p2, p2k = pT2.next()

                                def tr2():
                                    for h in range(4):
                                        i = T.transpose(p2[:, h, :], eb[:, h * 128:(h + 1) * 128], identb[:])
                                    return i
                                fw.op("pe", tr2, reads=[ek, "identb"], writes=[p2k])
                                tt, tk = tT.next()
                                fw.op("act", lambda: S_.copy(tt[:], p2[:, 0:4, :]), reads=[p2k], writes=[tk])
                                fw.dma("pool", s_GT[:, :, tok].rearrange("h p t -> p h t"), tt[:], reads=[tk])
                            else:
                                eb, ek = evb.next()
                                fw.op("act", lambda: S_.activation(out=eb[:], in_=pzt[:], func=AF.Sigmoid), reads=[pzk], writes=[ek])
                                dst = s_mgh if g < 11 else s_mgd
                                c0 = ((g - 9) % 2) * 512
                                fw.dma("pool", dst[tok, c0:c0 + 512], eb[:], reads=[ek])
                fw.barrier()

            with contextlib.ExitStack() as ph:
                def A(name, shape, dt):
                    return ph.enter_context(nc.sbuf_tensor(f"{name}_c{l}", shape, dt))

                def PS(name, shape, dt):
                    return ph.enter_context(nc.psum_tensor(f"{name}_c{l}", shape, dt))

                Lmax = max(segs)
                KTh = A("KTh", [128, Lmax], BF16)
                Vh = A("Vh", [128, Lmax // 128, 128], BF16)
                QTb = Ring(A, "QTb", 2, [128, 512], BF16)
                GTb = Ring(A, "GTb", 2, [128, 512], BF16)
                PT = Ring(A, "PT", 3, [128, 2, 512], BF16)
                rr = Ring(A, "rr", 2, [128, 512], F32)
                om = Ring(A, "om", 2, [128, 512], F32)
                oT = A("oT", [128, 512], F32)
                sq = A("sq", [128, 512], BF16)
                rn = A("rn", [128, 512], F32)
                of = A("of", [128, 512], F32)
                ofb = Ring(A, "ofb", 2, [128, 512], BF16)
                pS = Ring(PS, "pS", 3, [128, 2, 512], F32)
                pO = [PS("pO0", [128, 512], F32), PS("pO1", [128, 512], F32)]
                accT = A("accT", [128, 2, 512], F32)
                onesf = A("onesf", [128, 128], F32)
                fw.op("dve", lambda: V.memset(onesf[:], 1.0), writes=["onesf"])

                for sg, L in enumerate(segs):
                    so = offs[sg]
                    nt = L // 128
                    for h in range(4):
                        fw.dma("sp", KTh[:, 0:L], s_KT[h, :, so:so + L], writes=["KTh"])
                        for t0 in range(0, nt, 16):
                            t1 = min(nt, t0 + 16)
                            fw.dma("sp", Vh[:, t0:t1, :], s_V[so + t0 * 128:so + t1 * 128, h * 128:(h + 1) * 128].rearrange("(t p) c -> p t c", p=128), writes=["Vh"])
                        for qb in range(L // 512):
                            qs = slice(so + qb * 512, so + (qb + 1) * 512)
                            qt, qk = QTb.next()
                            fw.dma("sp", qt[:], s_QT[h, :, qs], writes=[qk])
                            gt, gk = GTb.next()
                            fw.dma("sp", gt[:], s_GT[h, :, qs], writes=[gk])

                            def emit_qk(j):
                                pst, psk = pS.next()

                                def qk_():
                                    T.matmul(pst[:, 0, :], lhsT=KTh[0:64, j * 128:(j + 1) * 128], rhs=qt[0:64, :], start=True, stop=True, tile_position=(0, 0))
                                    return T.matmul(pst[:, 1, :], lhsT=KTh[64:128, j * 128:(j + 1) * 128], rhs=qt[64:128, :], start=True, stop=True, tile_position=(64, 0))
                                fw.op("pe", qk_, reads=["KTh", qk], writes=[psk])
                                return pst, psk

                            pend = emit_qk(0)
                            for j in range(nt):
                                pst, psk = pend
                                if j + 1 < nt:
                                    pend = emit_qk(j + 1)
                                ptt, ptk = PT.next()
                                fw.op("act", lambda: S_.activation(out=ptt[:], in_=pst[:], func=AF.Exp, scale=0.125), reads=[psk], writes=[ptk])

                                def pv_():
                                    T.matmul(pO[0][:], lhsT=Vh[:, j, :], rhs=ptt[:, 0, :], start=(j == 0), stop=(j == nt - 1))
                                    return T.matmul(pO[1][:], lhsT=Vh[:, j, :], rhs=ptt[:, 1, :], start=(j == 0), stop=(j == nt - 1))
                                fw.op("pe", pv_, reads=["Vh", ptk], writes=["pO0", "pO1"])
                                if j == 0:
                                    fw.op("dve", lambda: V.tensor_copy(accT[:], ptt[:]), reads=[ptk], writes=["acc"])
                                else:
                                    fw.op("dve", lambda: V.tensor_tensor(accT[:], accT[:], ptt[:], ALU.add), reads=[ptk, "acc"], writes=["acc"])
                            oms = []
                            for m in range(2):
                                pst, psk = pS.next()
                                fw.op("pe", lambda: T.matmul(pst[:, 0, :], lhsT=onesf[:], rhs=accT[:, m, :], start=True, stop=True), reads=["onesf", "acc"], writes=[psk])
                                rt_, rk_ = rr.next()
                                fw.op("dve", lambda: V.reciprocal(rt_[:], pst[:, 0, :]), reads=[psk], writes=[rk_])
                                ot_, ok_ = om.next()
                                fw.op("dve", lambda: V.tensor_tensor(ot_[:], pO[m][:], rt_[:], ALU.mult), reads=[f"pO{m}", rk_], writes=[ok_])
                                oms.append((ot_, ok_))
                            (o0, o0k), (o1, o1k) = oms
                            fw.op("dve", lambda: V.scalar_tensor_tensor(oT[:], o1[:], neglam[:, l:l + 1], o0[:], ALU.mult, ALU.add), reads=[o0k, o1k, "neglam"], writes=["oT"])
                            fw.op("act", lambda: S_.activation(out=sq[:], in_=oT[:], func=AF.Square), reads=["oT"], writes=["sq"])
                            pst, psk = pS.next()
                            fw.op("pe", lambda: T.matmul(pst[:, 0, :], lhsT=onesb[:], rhs=sq[:], start=True, stop=True), reads=["sq", "onesb"], writes=[psk])
                            fw.op("dve", lambda: V.tensor_scalar(rn[:], pst[:, 0, :], 1.0 / 128, SUBLN_EPS, ALU.mult, ALU.add), reads=[psk], writes=["rn"])
                            fw.op("act", lambda: S_.activation(out=rn[:], in_=rn[:], func=AF.Ln), reads=["rn"], writes=["rn"])
                            fw.op("act", lambda: S_.activation(out=rn[:], in_=rn[:], func=AF.Exp, scale=-0.5), reads=["rn"], writes=["rn"])
                            fw.op("dve", lambda: V.tensor_tensor(of[:], oT[:], rn[:], ALU.mult), reads=["oT", "rn"], writes=["of"])
                            ob_, obk_ = ofb.next()
                            fw.op("dve", lambda: V.scalar_tensor_tensor(ob_[:], of[:], gsub[:, l:l + 1], gt[:], ALU.mult, ALU.mult), reads=["of", "gsub", gk], writes=[obk_])
                            fw.dma("pool", s_OT[h, :, qs], ob_[:], reads=[obk_])
                fw.barrier()

            with contextlib.ExitStack() as ph:
                def A(name, shape, dt):
                    return ph.enter_context(nc.sbuf_tensor(f"{name}_b{l}", shape, dt))

                def PS(name, shape, dt):
                    return ph.enter_context(nc.psum_tensor(f"{name}_b{l}", shape, dt))

                Wph = A("Wph", [128, 4, D], BF16)
                Wpd = A("Wpd", [128, 4, D], BF16)
                Wo = A("Wo", [128, 8, D], BF16)
                for k in range(4):
                    fw.dma("pool", Wph[:, k, :], w_proj_hg[l, k * 128:(k + 1) * 128, :], writes=["Wph"])
                    fw.dma("pool", Wpd[:, k, :], w_proj_da[l, k * 128:(k + 1) * 128, :], writes=["Wpd"])
                for k in range(8):
                    fw.dma("pool", Wo[:, k, :], w_out[l, k * 128:(k + 1) * 128, :], writes=["Wo"])
                GPb = A("GPb", [128, D], F32)
                Sst = [A("Sf", [128, 4, 128], F32), A("Sb", [128, 4, 128], F32)]
                Sbf = [[A("Sbff0", [128, 4, 128], BF16), A("Sbff1", [128, 4, 128], BF16)], [A("Sbfb0", [128, 4, 128], BF16), A("Sbfb1", [128, 4, 128], BF16)]]
                sbi = [0, 0]
                qh = Ring(A, "qh", 2, [128, 512], BF16)
                fin = Ring(A, "fin", 2, [128, 512], F32)
                vh = Ring(A, "vh", 2, [128, 512], BF16)
                gl = Ring(A, "gl", 2, [128, 512], F32)
                kk = Ring(A, "kk", 2, [128, 512], F32)
                ex = Ring(A, "ex", 8, [128, 512], F32)
                dec = Ring(A, "dec", 2, [128, 16], F32)
                pbc = Ring(A, "pbc", 2, [128, 512], F32)
                qt_ = Ring(A, "qt_", 2, [128, 512], BF16)
                kt_ = Ring(A, "kt_", 2, [128, 512], BF16)
                qi_ = Ring(A, "qi_", 2, [128, 512], BF16)
                kh_ = Ring(A, "kh_", 2, [128, 512], BF16)
                trs = Ring(A, "trs", 6, [128, 4, 128], BF16)
                Asb = Ring(A, "Asb", 2, [128, 4, 32], BF16)
                obuf = Ring(A, "obuf", 2, [128, 512], F32)
                obt = Ring(A, "obt", 2, [128, 512], F32)
                osum = A("osum", [128, 512], F32)
                sq4 = A("sq4", [128, 128], F32)
                ss4 = A("ss4", [128, 8], F32)
                ghb = Ring(A, "ghb", 2, [128, 512], BF16)
                gg = A("gg", [128, 512], F32)
                og = A("og", [128, 512], BF16)
                ogT = A("ogT", [128, 4, 128], BF16)
                odT = Ring(A, "odT", 2, [128, 4, 128], BF16)
                mgh = Ring(A, "mgh", 2, [128, D], BF16)
                mgd = Ring(A, "mgd", 2, [128, D], BF16)
                m1 = A("m1", [128, D], F32)
                m2 = A("m2", [128, D], F32)
                mb = A("mb", [128, D], BF16)
                mT = A("mT", [128, 8, 128], BF16)
                junk2 = A("junk2", [128, D], F32)
                xr = Ring(A, "xr", 2, [128, D], F32)
                yt = Ring(A, "yt", 2, [128, D], F32)
                pb = [PS(f"pb{i}", [128, 512], F32) for i in range(3)]
                pAd = PS("pAd", [128, 512], F32)
                pA = pAd[:, 0:128].rearrange("p (h t) -> p h t", h=4)
                pdec = pAd[:, 128:144]
                po = PS("po", [128, 4, 128], F32)
                pStR = Ring(PS, "pSt", 2, [128, 4, 128], F32)
                pTr = PS("pTr", [128, 8, 128], BF16)

                def hg_pre(so, t, d):
                    tok = slice(so + t * 128, so + (t + 1) * 128)
                    q_, qk_ = qh.next()
                    fw.dma("sp", q_[:], s_q[tok, :], writes=[qk_])
                    f_, fk_ = fin.next()
                    fw.dma("sp", f_[:], s_f[d][tok, :], writes=[fk_])
                    v_, vk_ = vh.next()
                    fw.dma("sp", v_[:], s_v[tok, :], writes=[vk_])
                    g_, gk_ = gl.next()
                    fw.op("act", lambda: S_.activation(out=g_[:], in_=f_[:], func=AF.Ln), reads=[fk_], writes=[gk_])
                    k_, kk_ = kk.next()
                    fw.op("dve", lambda: V.tensor_scalar(k_[:], f_[:], -1.0, 1.0, ALU.mult, ALU.add), reads=[fk_], writes=[kk_])
                    for i in range(3):
                        fw.op("pe", lambda: T.matmul(pb[i][:], lhsT=mats[:, 3 * d + i, :], rhs=g_[:], start=True, stop=True), reads=[gk_, "mats"], writes=[f"pb{i}"])

                    def dmm():
                        for h in range(4):
                            i_ = T.matmul(pdec[:, 4 * h:4 * h + 4], lhsT=g_[:, h * 128:(h + 1) * 128], rhs=ind[:], start=True, stop=True)
                        return i_
                    fw.op("pe", dmm, reads=[gk_, "ind"], writes=["pdec"])
                    es = [ex.next() for _ in range(4)]
                    pc_, pck_ = pbc.next()
                    fw.op("dve", lambda: V.tensor_scalar(pc_[:], pb[0][:], 43.0, -43.0, ALU.min, ALU.max), reads=["pb0"], writes=[pck_])
                    fw.op("act", lambda: S_.activation(out=es[0][0][:], in_=pc_[:], func=AF.Exp), reads=[pck_], writes=[es[0][1]])
                    fw.op("act", lambda: S_.activation(out=es[1][0][:], in_=pc_[:], func=AF.Exp, scale=-1.0), reads=[pck_], writes=[es[1][1]])
                    fw.op("act", lambda: S_.activation(out=es[2][0][:], in_=pb[1][:], func=AF.Exp), reads=["pb1"], writes=[es[2][1]])
                    fw.op("act", lambda: S_.activation(out=es[3][0][:], in_=pb[2][:], func=AF.Exp), reads=["pb2"], writes=[es[3][1]])
                    dc, dck = dec.next()
                    fw.op("act", lambda: S_.activation(out=dc[:], in_=pdec[:, 0:16], func=AF.Exp), reads=["pdec"], writes=[dck])
                    a_, ak_ = qt_.next()
                    fw.op("dve", lambda: V.tensor_tensor(a_[:], q_[:], es[0][0][:], ALU.mult), reads=[qk_, es[0][1]], writes=[ak_])
                    b_, bk_ = kt_.next()
                    fw.op("dve", lambda: V.tensor_tensor(b_[:], k_[:], es[1][0][:], ALU.mult), reads=[kk_, es[1][1]], writes=[bk_])
                    c_, ck_ = qi_.next()
                    fw.op("dve", lambda: V.tensor_tensor(c_[:], q_[:], es[2][0][:], ALU.mult), reads=[qk_, es[2][1]], writes=[ck_])
                    e_, ek_ = kh_.next()
                    fw.op("dve", lambda: V.tensor_tensor(e_[:], k_[:], es[3][0][:], ALU.mult), reads=[kk_, es[3][1]], writes=[ek_])
                    tts = []
                    for src, srck in ((a_, ak_), (b_, bk_), (c_, ck_)):
                        def tr():
                            for h in range(4):
                                i_ = T.transpose(pTr[:, h, :], src[:, h * 128:(h + 1) * 128], identb[:])
                            return i_
                        fw.op("pe", tr, reads=[srck, "identb"], writes=["pTr"])
                        tt, tk = trs.next()
                        fw.op("act", lambda: S_.copy(tt[:], pTr[:, 0:4, :]), reads=["pTr"], writes=[tk])
                        tts.append((tt, tk))
                    (qtT, qtTk), (ktT, ktTk), (qiT, qiTk) = tts
                    at, atk = Asb.next()

                    def sc():
                        for c in range(4):
                            rc = slice(32 * c, 32 * c + 32)
                            for h in range(4):
                                i_ = T.matmul(pA[rc, h, 0:32], lhsT=ktT[:, h, rc], rhs=qtT[:, h, rc], start=True, stop=True, tile_position=(0, 32 * c))
                        return i_
                    fw.op("pe", sc, reads=[qtTk, ktTk], writes=["pA"])
                    fw.op("dve", lambda: V.tensor_tensor(at[:], pA[:, :, 0:32], masks[:, d, :].rearrange("p (h t) -> p h t", h=4), ALU.mult), reads=["pA", "masks"], writes=[atk])
                    return dict(v_=v_, vk_=vk_, dc=dc, dck=dck, at=at, atk=atk, qiT=qiT, qiTk=qiTk, e_=e_, ek_=ek_)

                def hg_chain(cx, d):
                    v_, vk_, dc, dck, at, atk, qiT, qiTk, e_, ek_ = (cx[k] for k in ("v_", "vk_", "dc", "dck", "at", "atk", "qiT", "qiTk", "e_", "ek_"))
                    St = Sst[d]
                    order = (0, 1, 2, 3) if d == 0 else (3, 2, 1, 0)
                    psts = {}

                    def emit_sm(c):
                        rc = slice(32 * c, 32 * c + 32)
                        pSt, pStk = pStR.next()

                        def sm_():
                            for h in range(4):
                                i_ = T.matmul(pSt[:, h, :], lhsT=e_[rc, h * 128:(h + 1) * 128], rhs=v_[rc, h * 128:(h + 1) * 128], start=True, stop=True, tile_position=(32 * c, 0))
                            return i_
                        fw.op("pe", sm_, reads=[ek_, vk_], writes=[pStk])
                        psts[c] = (pSt, pStk)

                    emit_sm(order[0])
                    emit_sm(order[1])
                    for n_, c in enumerate(order):
                        rc = slice(32 * c, 32 * c + 32)
                        Sb_ = Sbf[d][sbi[d]]
                        Sbk_ = f"Sbf{d}{sbi[d]}"
                        sbi[d] ^= 1
                        Sn_ = Sbf[d][sbi[d]]
                        Snk_ = f"Sbf{d}{sbi[d]}"
                        pSt, pStk = psts[c]

                        def om_():
                            for h in range(4):
                                T.matmul(po[rc, h, :], lhsT=at[rc, h, :], rhs=v_[rc, h * 128:(h + 1) * 128], start=True, stop=False, tile_position=(32 * c, 32 * c))
                                i_ = T.matmul(po[rc, h, :], lhsT=qiT[:, h, rc], rhs=Sb_[:, h, :], start=False, stop=True, tile_position=(0, 32 * c))
                            return i_
                        fw.op("pe", om_, reads=[atk, vk_, qiTk, Sbk_], writes=["po"])

                        def su_():
                            for h in range(4):
                                i_ = V.scalar_tensor_tensor(St[:, h, :], St[:, h, :], dc[:, 4 * h + c:4 * h + c + 1], pSt[:, h, :], ALU.mult, ALU.add)
                            return i_
                        fw.op("dve", su_, reads=[pStk, dck, f"S{d}"], writes=[f"S{d}"])
                        fw.op("act", lambda: S_.copy(Sn_[:], St[:]), reads=[f"S{d}"], writes=[Snk_])
                        if n_ + 2 < 4:
                            emit_sm(order[n_ + 2])

                for sg, L in enumerate(segs):
                    so = offs[sg]
                    nt = L // 128
                    r0 = (l * nseg + sg) * 3
                    fw.dma("sp", GPb[:], s_mod[r0 + 2:r0 + 3, :].partition_broadcast(128), writes=["GPb"])
                    for d in range(2):
                        sbi[d] = 0
                        fw.op("dve", lambda: V.memset(Sst[d][:], 0.0), writes=[f"S{d}"])
                        fw.op("dve", lambda: V.memset(Sbf[d][0][:], 0.0), writes=[f"Sbf{d}0"])
                    order = list(reversed(range(nt)))
                    cx = hg_pre(so, order[0], 1)
                    for n_, t in enumerate(order):
                        tok = slice(so + t * 128, so + (t + 1) * 128)
                        nx = hg_pre(so, order[n_ + 1], 1) if n_ + 1 < nt else None
                        hg_chain(cx, 1)
                        cx = nx
                        ob_, obk_ = obuf.next()
                        fw.op("act", lambda: S_.copy(ob_[:], po[:].rearrange("p h v -> p (h v)")), reads=["po"], writes=[obk_])
                        fw.dma("pool", s_ob[tok, :], ob_[:], reads=[obk_], writes=[f"sob{so // 128 + t}"])
                    cx = hg_pre(so, 0, 0)
                    for t in range(nt):
                        tok = slice(so + t * 128, so + (t + 1) * 128)
                        nx = hg_pre(so, t + 1, 0) if t + 1 < nt else None
                        hg_chain(cx, 0)
                        cx = nx
                        obl, oblk = obt.next()
                        fw.dma("sp", obl[:], s_ob[tok, :], reads=[f"sob{so // 128 + t}"], writes=[oblk])
                        fw.op("dve", lambda: V.tensor_tensor(osum[:], po[:].rearrange("p h v -> p (h v)"), obl[:], ALU.add), reads=["po", oblk], writes=["osum"])
                        fw.op("dve", lambda: V.memset(ss4[:, 0:4], 0.0), writes=["ss4"])

                        def sqs():
                            for h in range(4):
                                i_ = S_.activation(out=sq4[:], in_=osum[:, h * 128:(h + 1) * 128], func=AF.Square, accum_out=ss4[:, h:h + 1])
                            return i_
                        fw.op("act", sqs, reads=["osum", "ss4"], writes=["sq4", "ss4"])
                        fw.op("dve", lambda: V.tensor_scalar(ss4[:, 4:8], ss4[:, 0:4], 1.0 / 128, NORM_EPS, ALU.mult, ALU.add), reads=["ss4"], writes=["ss4b"])
                        fw.op("act", lambda: S_.activation(out=ss4[:, 4:8], in_=ss4[:, 4:8], func=AF.Ln), reads=["ss4b"], writes=["ss4b"])
                        fw.op("act", lambda: S_.activation(out=ss4[:, 4:8], in_=ss4[:, 4:8], func=AF.Exp, scale=-0.5), reads=["ss4b"], writes=["ss4b"])
                        gh_, ghk_ = ghb.next()
                        fw.dma("sp", gh_[:], s_gh[tok, :], writes=[ghk_])
                        fw.op("dve", lambda: V.tensor_tensor(gg[:], gh_[:], hgnb[:, l, :], ALU.mult), reads=[ghk_] + [f"hgnb{l}{h}" for h in range(4)], writes=["gg"])

                        def ogf():
                            for h in range(4):
                                i_ = V.scalar_tensor_tensor(og[:, h * 128:(h + 1) * 128], osum[:, h * 128:(h + 1) * 128], ss4[:, 4 + h:5 + h], gg[:, h * 128:(h + 1) * 128], ALU.mult, ALU.mult)
                            return i_
                        fw.op("dve", ogf, reads=["osum", "ss4b", "gg"], writes=["og"])

                        def tr():
                            for h in range(4):
                                i_ = T.transpose(pTr[:, h, :], og[:, h * 128:(h + 1) * 128], identb[:])
                            return i_
                        fw.op("pe", tr, reads=["og", "identb"], writes=["pTr"])
                        fw.op("act", lambda: S_.copy(ogT[:], pTr[:, 0:4, :]), reads=["pTr"], writes=["ogT"])
                        od_, odk_ = odT.next()
                        fw.dma("sp", od_[:], s_OT[:, :, tok].rearrange("h p t -> p h t"), writes=[odk_])
                        mh_, mhk_ = mgh.next()
                        fw.dma("sp", mh_[:], s_mgh[tok, :], writes=[mhk_])
                        md_, mdk_ = mgd.next()
                        fw.dma("sp", md_[:], s_mgd[tok, :], writes=[mdk_])
                        for (lt, ltk, W_, Wk_, mg_, mgk_, mo, mok) in ((ogT, "ogT", Wph, "Wph", mh_, mhk_, m1, "m1"), (od_, odk_, Wpd, "Wpd", md_, mdk_, m2, "m2")):
                            def pm():
                                for n in range(2):
                                    for h in range(4):
                                        i_ = T.matmul(pb[n][:], lhsT=lt[:, h, :], rhs=W_[:, h, n * 512:(n + 1) * 512], start=(h == 0), stop=(h == 3))
                                return i_
                            fw.op("pe", pm, reads=[ltk, Wk_], writes=["pb0", "pb1"])

                            def gm():
                                for n in range(2):
                                    i_ = V.tensor_tensor(mo[:, n * 512:(n + 1) * 512], pb[n][:], mg_[:, n * 512:(n + 1) * 512], ALU.mult)
                                return i_
                            fw.op("dve", gm, reads=["pb0", "pb1", mgk_], writes=[mok])
                        fw.op("dve", lambda: V.tensor_tensor(mb[:], m1[:], m2[:], ALU.add), reads=["m1", "m2"], writes=["mb"])

                        def tr8():
                            for k in range(8):
                                i_ = T.transpose(pTr[:, k, :], mb[:, k * 128:(k + 1) * 128], identb[:])
                            return i_
                        fw.op("pe", tr8, reads=["mb", "identb"], writes=["pTr"])
                        fw.op("act", lambda: S_.copy(mT[:], pTr[:]), reads=["pTr"], writes=["mT"])

                        def om2():
                            for n in range(2):
                                for k in range(8):
                                    i_ = T.matmul(pb[n][:], lhsT=mT[:, k, :], rhs=Wo[:, k, n * 512:(n + 1) * 512], start=(k == 0), stop=(k == 7))
                            return i_
                        fw.op("pe", om2, reads=["mT", "Wo"], writes=["pb0", "pb1"])
                        fw.op("dve", lambda: V.memset(ss4[:, 0:2], 0.0), writes=["ss4"])

                        def sq2():
                            for n in range(2):
                                i_ = S_.activation(out=junk2[:, n * 512:(n + 1) * 512], in_=pb[n][:], func=AF.Square, accum_out=ss4[:, n:n + 1])
                            return i_
                        fw.op("act", sq2, reads=["pb0", "pb1", "ss4"], writes=["junk2", "ss4"])
                        fw.op("dve", lambda: V.tensor_tensor(ss4[:, 2:3], ss4[:, 0:1], ss4[:, 1:2], ALU.add), reads=["ss4"], writes=["ss4c"])
                        fw.op("dve", lambda: V.tensor_scalar(ss4[:, 2:3], ss4[:, 2:3], 1.0 / D, NORM_EPS, ALU.mult, ALU.add), reads=["ss4c"], writes=["ss4c"])
                        fw.op("act", lambda: S_.activation(out=ss4[:, 2:3], in_=ss4[:, 2:3], func=AF.Ln), reads=["ss4c"], writes=["ss4c"])
                        fw.op("act", lambda: S_.activation(out=ss4[:, 3:4], in_=ss4[:, 2:3], func=AF.Exp, scale=-0.5), reads=["ss4c"], writes=["ss4d"])
                        x_, xk_ = xr.next()
                        fw.dma("sp", x_[:], xsrc[tok, :], writes=[xk_])
                        y_, yk_ = yt.next()

                        def yo():
                            for n in range(2):
                                i_ = V.scalar_tensor_tensor(y_[:, n * 512:(n + 1) * 512], pb[n][:], ss4[:, 3:4], GPb[:, n * 512:(n + 1) * 512], ALU.mult, ALU.mult)
                            return i_
                        fw.op("dve", yo, reads=["pb0", "pb1", "ss4d", "GPb"], writes=[yk_])
                        fw.op("dve", lambda: V.tensor_tensor(y_[:], y_[:], x_[:], ALU.add), reads=[yk_, xk_], writes=[yk_])
                        fw.dma("pool", ydst[tok, :], y_[:], reads=[yk_])
                fw.barrier()
    return nc


_CACHE = {}


def run(segs, x_cores, c_cores, weights, n_cores=N_CORES, nl=DEPTH, debug=False):
    key = (tuple(segs), nl, debug)
    if key not in _CACHE:
        _CACHE[key] = (build(list(segs), nl, debug), host_consts(max(segs) // 128))
    nc, kc = _CACHE[key]
    f = lambda a: np.ascontiguousarray(np.asarray(a, dtype=np.float32))
    common = {
        "w_ada": f(weights["w_ada"]), "b_ada": f(weights["b_ada"]), "g_pre": f(weights["g_pre"]), "g_post": f(weights["g_post"]),
        "w_in": f(weights["w_in"]), "hg_lb": f(weights["hg_lower_bounds"]).reshape(1, -1), "hg_norm_g": f(weights["hg_norm_g"]),
        "da_lambda": f(weights["da_lambda"]).reshape(DEPTH, 256), "da_subln_g": f(weights["da_subln_g"]),
        "w_proj_hg": f(weights["w_proj_hg"]), "w_proj_da": f(weights["w_proj_da"]), "w_out": f(weights["w_out"]),
        "k_ident": kc["ident"], "k_mats": kc["mats"], "k_ind": kc["ind"], "k_masks": kc["masks"], "k_rope": kc["rope"],
    }
    in_maps = []
    for i in range(n_cores):
        c = f(c_cores[i])
        c_lay = np.ascontiguousarray(c.reshape(len(segs), 8, 128).transpose(0, 2, 1))
        in_maps.append({"x": f(x_cores[i]), "c": c_lay, **common})
    res = run_bass_kernel_spmd(nc, in_maps, core_ids=list(range(n_cores)))
    if debug:
        return res.results
    return [r["y"] for r in res.results]


def kernel(x_prompt, x_sample, c_prompt, c_sample, w_ada, b_ada, g_pre, g_post, w_in, hg_lower_bounds, hg_norm_g,
           da_lambda, da_subln_g, w_proj_hg, w_proj_da, w_out):
    x_prompt = np.asarray(x_prompt, np.float32)
    x_sample = np.asarray(x_sample, np.float32)
    c_prompt = np.asarray(c_prompt, np.float32)
    c_sample = np.asarray(c_sample, np.float32)
    B, L, _ = x_prompt.shape
    Bs, Ls, _ = x_sample.shape
    per = B // N_CORES
    segs = [L] * per + [Ls]
    weights = dict(w_ada=w_ada, b_ada=b_ada, g_pre=g_pre, g_post=g_post, w_in=w_in, hg_lower_bounds=hg_lower_bounds,
                   hg_norm_g=hg_norm_g, da_lambda=da_lambda, da_subln_g=da_subln_g, w_proj_hg=w_proj_hg,
                   w_proj_da=w_proj_da, w_out=w_out)
    xc, cc = [], []
    for i in range(N_CORES):
        xs = [x_prompt[i * per + j] for j in range(per)] + [x_sample[i % Bs]]
        cs = [c_prompt[i * per + j] for j in range(per)] + [c_sample[i % Bs]]
        xc.append(np.concatenate(xs, axis=0))
        cc.append(np.stack(cs, axis=0))
    ys = run(segs, xc, cc, weights)
    y_prompt = np.empty_like(x_prompt)
    y_sample = np.empty_like(x_sample)
    for i in range(N_CORES):
        for j in range(per):
            y_prompt[i * per + j] = ys[i][j * L:(j + 1) * L]
    for s in range(Bs):
        y_sample[s] = ys[s][per * L: per * L + Ls]
    return (y_prompt, y_sample)
```
